# Optimizing a Trainium2 kernel written in Bass

```python
import jax, jax.numpy as jnp
from jax import lax
import numpy as np

D_MODEL = 1024
BATCH = 16
SEQ = 2048
DEPTH = 1

D_CONV = 1024
CONV_WIDTH = 31
RWKV_HEAD = 64
RWKV_HEADS = D_MODEL // RWKV_HEAD
D_RWKV = RWKV_HEADS * RWKV_HEAD
DECAY_LORA = 64
AAA_LORA = 64
TS_WIDTH = 3 * D_RWKV + DECAY_LORA + AAA_LORA
SPLITS = (D_CONV, 2 * D_CONV, 3 * D_CONV, 3 * D_CONV + TS_WIDTH, 3 * D_CONV + TS_WIDTH + D_RWKV, 3 * D_CONV + TS_WIDTH + D_RWKV + D_MODEL)
D_IN = 3 * D_CONV + TS_WIDTH + D_RWKV + 2 * D_MODEL
RWKV_SPLITS = (D_RWKV, 2 * D_RWKV, 3 * D_RWKV, 3 * D_RWKV + DECAY_LORA)

RMS_EPS = 1e-6
LN_EPS = 1e-5
GN_EPS = 64e-5
L2_EPS = 1e-12

kernel_name = 'hybrid_conformer_conv_rwkv7_adaln'


def _rmsnorm(x, g):
    xf = x.astype(jnp.float32)
    y = xf * lax.rsqrt(jnp.mean(xf * xf, axis=-1, keepdims=True) + RMS_EPS)
    return (y * g.astype(jnp.float32)).astype(x.dtype)


def _layernorm(x, g, b, eps):
    xf = x.astype(jnp.float32)
    mu = jnp.mean(xf, axis=-1, keepdims=True)
    var = jnp.mean(jnp.square(xf - mu), axis=-1, keepdims=True)
    y = (xf - mu) * lax.rsqrt(var + eps) * g.astype(jnp.float32) + b.astype(jnp.float32)
    return y.astype(x.dtype)


def _causal_depthwise_conv(u, w, b):
    y = lax.conv_general_dilated(
        u, w[:, None, :].astype(u.dtype), window_strides=(1,),
        padding=((CONV_WIDTH - 1, 0),), dimension_numbers=('NWC', 'WIO', 'NWC'),
        feature_group_count=u.shape[-1])
    return y + b


def _token_shift(z, mu):
    prev = jnp.pad(z, ((0, 0), (1, 0), (0, 0)))[:, :-1]
    return z + mu * (prev - z)


def _rwkv7_scan(r, decay, k, v, kk, b):
    bsz, _, h, n = r.shape

    def step(state, inp):
        r_t, w_t, k_t, v_t, kk_t, b_t = inp
        sk = jnp.einsum('bhvk,bhk->bhv', state, kk_t)
        state = (state * w_t[:, :, None, :] - sk[..., None] * b_t[:, :, None, :]
                 + v_t[..., None] * k_t[:, :, None, :])
        return state, jnp.einsum('bhvk,bhk->bhv', state, r_t)

    xs = tuple(jnp.moveaxis(t, 1, 0) for t in (r, decay, k, v, kk, b))
    s0 = jnp.zeros((bsz, h, n, n), jnp.float32)
    _, out = lax.scan(step, s0, xs)
    return jnp.moveaxis(out, 0, 1)


def _rwkv7_branch(ts, og, mu, w0, w2, a0, a2, k_k, k_a, r_k, gn_g, gn_b, w_o):
    f32 = jnp.float32
    bsz, seq, _ = ts.shape
    hd = (bsz, seq, RWKV_HEADS, RWKV_HEAD)
    r, k, v, w_low, a_low = jnp.split(_token_shift(ts, mu), RWKV_SPLITS, axis=-1)
    w_pre = (w0 + jnp.tanh(w_low) @ w2).astype(f32)
    decay = jnp.exp(-jnp.exp(-jax.nn.softplus(-w_pre) - 0.5))
    a = jax.nn.sigmoid((a0 + a_low @ a2).astype(f32))
    kk = (k * k_k).astype(f32).reshape(hd)
    kk = kk / jnp.maximum(jnp.sqrt(jnp.sum(kk * kk, axis=-1, keepdims=True)), L2_EPS)
    k = (k.astype(f32) * (1.0 + (a - 1.0) * k_a.astype(f32))).reshape(hd)
    r = r.astype(f32).reshape(hd)
    v = v.astype(f32).reshape(hd)
    a = a.reshape(hd)
    o = _rwkv7_scan(r, decay.reshape(hd), k, v, kk, kk * a)
    o = _layernorm(o, gn_g.reshape(RWKV_HEADS, RWKV_HEAD), gn_b.reshape(RWKV_HEADS, RWKV_HEAD), GN_EPS)
    o = o + jnp.sum(r * k * r_k.astype(f32), axis=-1, keepdims=True) * v
    o = o.reshape(bsz, seq, D_RWKV).astype(og.dtype) * jax.nn.silu(og)
    return o @ w_o


def _conv_branch(val, gate, og, conv_k, conv_b, ln_g, ln_b, w_o):
    u = val * jax.nn.sigmoid(gate)
    u = _causal_depthwise_conv(u, conv_k, conv_b)
    u = _layernorm(u, ln_g, ln_b, LN_EPS)
    u = jax.nn.silu(u) * jax.nn.silu(og)
    return u @ w_o


def setup_inputs(seed: int = 0) -> dict:
    key = jax.random.key(seed)
    ks = jax.random.split(key, 24)
    f32 = jnp.float32
    nrm = lambda k, shape, s: jax.random.normal(k, shape, f32) * s
    L = DEPTH
    return {
        'x': nrm(ks[0], (BATCH, SEQ, D_MODEL), 1.0),
        'c': nrm(ks[1], (BATCH, D_MODEL), 1.0),
        'ada_w': nrm(ks[2], (L, D_MODEL, 3 * D_MODEL), 0.5 * D_MODEL ** -0.5),
        'ada_b': nrm(ks[3], (L, 3 * D_MODEL), 0.01),
        'norm_g': 1.0 + nrm(ks[4], (L, D_MODEL), 0.1),
        'w_in': nrm(ks[5], (L, D_MODEL, D_IN), D_MODEL ** -0.5),
        'conv_k': nrm(ks[6], (L, CONV_WIDTH, D_CONV), CONV_WIDTH ** -0.5),
        'conv_b': nrm(ks[7], (L, D_CONV), 0.01),
        'conv_ln_g': 1.0 + nrm(ks[8], (L, D_CONV), 0.1),
        'conv_ln_b': nrm(ks[9], (L, D_CONV), 0.01),
        'w_conv_out': nrm(ks[10], (L, D_CONV, D_MODEL), D_CONV ** -0.5),
        'rwkv_mu': jax.random.uniform(ks[11], (L, TS_WIDTH), f32, 0.0, 1.0),
        'rwkv_w0': jax.random.uniform(ks[12], (L, D_RWKV), f32, -6.0, -1.0),
        'rwkv_w2': nrm(ks[13], (L, DECAY_LORA, D_RWKV), 0.1 * DECAY_LORA ** -0.5),
        'rwkv_a0': nrm(ks[14], (L, D_RWKV), 0.1),
        'rwkv_a2': nrm(ks[15], (L, AAA_LORA, D_RWKV), 0.5 * AAA_LORA ** -0.5),
        'rwkv_k_k': 0.85 + nrm(ks[16], (L, D_RWKV), 0.05),
        'rwkv_k_a': 1.0 + nrm(ks[17], (L, D_RWKV), 0.05),
        'rwkv_r_k': nrm(ks[18], (L, RWKV_HEADS, RWKV_HEAD), 0.1),
        'rwkv_gn_g': 1.0 + nrm(ks[19], (L, D_RWKV), 0.1),
        'rwkv_gn_b': nrm(ks[20], (L, D_RWKV), 0.01),
        'w_rwkv_out': nrm(ks[21], (L, D_RWKV, D_MODEL), D_RWKV ** -0.5),
        'w_out': nrm(ks[22], (L, D_MODEL, D_MODEL), D_MODEL ** -0.5),
        'final_g': 1.0 + nrm(ks[23], (D_MODEL,), 0.1),
    }


def reference(x, c, ada_w, ada_b, norm_g, w_in, conv_k, conv_b, conv_ln_g, conv_ln_b, w_conv_out,
              rwkv_mu, rwkv_w0, rwkv_w2, rwkv_a0, rwkv_a2, rwkv_k_k, rwkv_k_a, rwkv_r_k,
              rwkv_gn_g, rwkv_gn_b, w_rwkv_out, w_out, final_g):
    for l in range(DEPTH):
        mod = jnp.einsum('bd,de->be', jax.nn.silu(c), ada_w[l]) + ada_b[l]
        shift, scale, gate = jnp.split(mod, 3, axis=-1)
        h = _rmsnorm(x, norm_g[l]) * (1.0 + scale[:, None, :]) + shift[:, None, :]
        p = jnp.einsum('bsd,de->bse', h, w_in[l])
        c_val, c_gate, c_og, ts, r_og, g_conv, g_rwkv = jnp.split(p, SPLITS, axis=-1)
        y_conv = _conv_branch(c_val, c_gate, c_og, conv_k[l], conv_b[l], conv_ln_g[l], conv_ln_b[l],
                              w_conv_out[l])
        y_rwkv = _rwkv7_branch(ts, r_og, rwkv_mu[l], rwkv_w0[l], rwkv_w2[l], rwkv_a0[l], rwkv_a2[l],
                               rwkv_k_k[l], rwkv_k_a[l], rwkv_r_k[l], rwkv_gn_g[l], rwkv_gn_b[l],
                               w_rwkv_out[l])
        m = jax.nn.sigmoid(g_conv) * y_conv + jax.nn.sigmoid(g_rwkv) * y_rwkv
        out = jnp.einsum('bsd,de->bse', m, w_out[l])
        x = x + gate[:, None, :] * out
    return _rmsnorm(x, final_g)
```

```python
import contextlib
import numpy as np
import ml_dtypes
import concourse.bass as bass
import concourse.mybir as mybir
from concourse.bass_utils import run_bass_kernel_spmd

F32 = mybir.dt.float32
BF16 = mybir.dt.bfloat16
AF = mybir.ActivationFunctionType
ALU = mybir.AluOpType

D = 1024
KC = 8
CW = 31
NCT = 73
CT_VAL, CT_GATE, CT_OG, CT_R, CT_K, CT_V, CT_LORA, CT_ROG, CT_GC, CT_GR = 0, 8, 16, 24, 32, 40, 48, 49, 57, 65
C0 = float(np.exp(-0.5))
TB = 512
CH = 128

ENG = ("pe", "act", "dve", "pool", "sp")
EPOCH = 8000


class Buf:
    __slots__ = ("name", "w", "rs")

    def __init__(self, name):
        self.name = name
        self.w = None
        self.rs = []


class Prog:
    def __init__(self, nc, stack, dry=False):
        self.nc, self.stack, self.dry = nc, stack, dry
        self.q = {e: [] for e in ENG}
        self.cnt = {e: 0 for e in ENG}
        self.sems = {e: [] for e in ENG}
        self.seen = {e: {} for e in ENG}
        self.dsem = {}
        self.nsem = 0

    def _sem(self, name):
        self.nsem += 1
        return self.stack.enter_context(self.nc.semaphore(name))

    def _waits(self, e, reads, writes):
        evs = []
        for b in reads:
            if b.w is not None:
                evs.append((b.w, 0))
        for b in writes:
            if b.w is not None:
                evs.append((b.w, 1))
            for r in b.rs:
                evs.append((r, 2))
        waits = {}
        seen = self.seen[e]
        for (sem, val, eng, idx), kind in evs:
            if eng == e and e == "pe":
                continue
            key = id(sem)
            if seen.get(key, 0) >= val:
                continue
            if key in waits and waits[key][1] >= val:
                continue
            waits[key] = (sem, val)
        for key, (sem, val) in waits.items():
            seen[key] = val
        return list(waits.values())

    def _post(self, ev, reads, writes):
        for b in reads:
            b.rs.append(ev)
        for b in writes:
            b.w = ev
            b.rs = []

    def op(self, e, fn, reads=(), writes=()):
        if self.dry:
            return
        waits = self._waits(e, reads, writes)
        i = self.cnt[e]
        ep = i // EPOCH
        if ep >= len(self.sems[e]):
            self.sems[e].append(self._sem("s_%s_%d" % (e, ep)))
        sem = self.sems[e][ep]
        ev = (sem, i % EPOCH + 1, e, i)
        self.cnt[e] += 1
        self.q[e].append((waits, fn, sem, 1))
        self._post(ev, reads, writes)

    def dma(self, e, fn, reads, writes, key):
        if self.dry:
            return
        waits = self._waits(e, reads, writes)
        if key not in self.dsem:
            self.dsem[key] = [self._sem("d_" + key), 0]
        st = self.dsem[key]
        st[1] += 16
        ev = (st[0], st[1], None, -1)
        self.q[e].append((waits, fn, st[0], 16))
        self._post(ev, reads, writes)

    def final_wait(self, e, bufs):
        if self.dry:
            return
        waits = self._waits(e, bufs, bufs)
        self.q[e].append((waits, None, None, 0))

    def emit(self):
        nc = self.nc
        with nc.Block() as block:
            def mk(e):
                def body(eng):
                    for waits, fn, sem, inc in self.q[e]:
                        for s, v in waits:
                            eng.wait_ge(s, v)
                        if fn is not None:
                            fn(eng).then_inc(sem, inc)
                return body
            block.tensor(mk("pe"))
            block.scalar(mk("act"))
            block.vector(mk("dve"))
            block.gpsimd(mk("pool"))
            block.sync(mk("sp"))


class RPool:
    def __init__(self, regs):
        self.regs = regs
        self.held = [False] * len(regs)
        self.nxt = 0

    def get(self):
        n = len(self.regs)
        for k in range(n):
            i = (self.nxt + k) % n
            if not self.held[i]:
                self.held[i] = True
                self.nxt = (i + 1) % n
                return i
        raise RuntimeError("PSUM pool exhausted")

    def rel(self, i):
        assert self.held[i]
        self.held[i] = False


def vec_layout(nseq):
    off = {}
    n = 0
    for name, w in (("ngrep", 8 * nseq), ("ada_b", 24), ("conv_b", 8), ("ln_g", 8), ("ln_b", 8),
                    ("conv_k", 8 * CW), ("mu", 25), ("w0", 8), ("a0", 8), ("k_k", 8), ("k_a", 8),
                    ("r_k", 8), ("gn_g", 8), ("gn_b", 8)):
        off[name] = n
        n += w
    return off, n


def build_program(nseq, T, TS, dbg=None):
    assert T % TS == 0 and TS % TB == 0
    NTB = TS // TB
    NSEG = T // TS
    NTT = TS // 128
    NTOK = nseq * T
    VO, NV = vec_layout(nseq)
    nc = bass.Bass("TRN2", target_bir_lowering=False)

    def din(name, shape, dt=F32):
        return nc.dram_tensor(name, list(shape), dt, kind="ExternalInput").ap()

    d_x = din("x", [NTOK, D])
    d_cT = din("cT", [128, KC * nseq])
    d_adaw = din("adaw_t", [24, 128, D])
    d_win = din("win_t", [NCT, 128, D])
    d_wco = din("wco_t", [8, 128, D])
    d_wro = din("wro_t", [8, 128, D])
    d_wout = din("wout_r", [128, KC * D])
    d_wa2 = din("wa2", [128, D])
    d_vecs = din("vecs", [128, NV])
    d_fg = din("fg_bc", [128, D])
    d_cf = din("cst_f", [128, 1536])
    d_cb = din("cst_b", [128, 512], BF16)
    d_out = nc.dram_tensor("out", [NTOK, D], F32, kind="ExternalOutput").ap()
    d_dbg = {}
    if dbg:
        for name, shape in dbg.items():
            d_dbg[name] = nc.dram_tensor("dbg_" + name, list(shape), F32, kind="ExternalOutput").ap()

    stack = contextlib.ExitStack()
    with stack:
        def sb(name, shape, dt=F32):
            return stack.enter_context(nc.sbuf_tensor("sb_" + name, list(shape), dt))

        hT = sb("hT", [128, KC, TS], BF16)
        mT = sb("mT", [128, KC, TS], BF16)
        cbuf = sb("cbuf", [128, KC, TS], BF16)
        NW, WLA = 8, 2
        wring = [sb("wr%d" % i, [128, KC, 128], BF16) for i in range(NW)]
        wout = sb("wout", [128, KC, D], BF16)
        wa2 = sb("wa2b", [128, D], BF16)
        tbig = sb("tbig", [128, 14 * TB])
        NXS = 3
        xring = [tbig[:, i * D:(i + 1) * D] for i in range(NXS)]
        xn = [tbig[:, (NXS + i) * D:(NXS + 1 + i) * D] for i in range(NXS)]
        gate_bc = sb("gate_bc", [128, D])
        fg_bc = sb("fgbc", [128, D])
        vecs = sb("vecs", [128, NV])
        cf = sb("cstf", [128, 1536])
        cb = sb("cstb", [128, 512], BF16)
        small = sb("small", [128, 64])
        scT = sb("scT", [128, KC * nseq])
        modT = sb("modT", [128, 24 * nseq])
        s1 = sb("s1", [128, KC * nseq])
        omu = sb("omu", [128, 25])
        omka = sb("omka", [128, 8])
        grep = sb("grep", [128, 128])
        ARENA = 53504 + 7168
        arena = sb("arena", [128, ARENA // 2], BF16)
        carve_pos = [0]

        def carve(shape, dt=BF16):
            n = int(np.prod(shape[1:])) * (2 if dt == BF16 else 4)
            a = carve_pos[0]
            assert a % 4 == 0 and a + n <= ARENA, (a, n)
            carve_pos[0] = a + ((n + 3) // 4) * 4
            ap = arena[:, a // 2:(a + n) // 2]
            if dt != BF16:
                ap = ap.bitcast(dt)
            if len(shape) == 3:
                ap = ap.rearrange("p (a b) -> p a b", b=shape[2])
            elif len(shape) == 4:
                ap = ap.rearrange("p (a b c) -> p a b c", b=shape[2], c=shape[3])
            return ap
        ub = [carve([128, 30 + TS]) for i in range(2)]
        ucarry = sb("ucarry", [128, 8, 30], BF16)
        dg = [carve([128, CW, 128]) for i in range(2)]
        acc1 = carve([128, TS], F32)
        acc2 = carve([128, TS], F32)
        carve_pos[0] = 0
        t512 = [tbig[:, i * TB:(i + 1) * TB] for i in range(14)]
        b512 = [sb("b512_%d" % i, [128, TB], BF16) for i in range(7)]
        Abl = carve([128, 2 + TS])
        Ab = carve([128, 3, 2 + TS])
        abcarry = sb("abcarry", [128, 8, 3], BF16)
        ablc = sb("ablc", [128, 2], BF16)
        la = carve([128, TS])
        Hst = sb("Hst", [128, 8, 64])
        Hbf = sb("Hbf", [128, 64], BF16)
        PCt2 = [sb("PCt%d" % i, [128, 4]) for i in range(2)]
        ar2 = [carve([128, 4, 2, 128]) for _ in range(2)]
        VT2 = [carve([128, 4, 128]) for _ in range(2)]
        btT2 = [carve([128, 4, 128]) for _ in range(2)]
        ktT2 = [carve([128, 4, 128]) for _ in range(2)]
        Ab42 = [[carve([128, 4, 128]) for i in range(8)] for _ in range(2)]
        Xfin_par = [carve([128, 8, 128]) for _ in range(2)]
        Xfin2 = [[Xfin_par[p_][:, 2 * (g_ % 4) + g_ // 4, :] for g_ in range(8)] for p_ in range(2)]
        Xtmp_par = [carve([128, 8, 128]) for _ in range(2)]
        Xtmp = [[Xtmp_par[j][:, i, :] for j in range(2)] for i in range(8)]
        NTb = [carve([128, 128]) for i in range(8)]
        PQs_par = [carve([128, 8, 256]) for _ in range(2)]
        PQs = [[PQs_par[j][:, i, :] for j in range(2)] for i in range(8)]
        RHSs = carve([128, 128])
        Us = carve([128, 128])
        No2 = [[carve([128, 128]) for i in range(8)] for _ in range(2)]
        W1s = carve([128, 128])
        R2s = carve([128, 128])
        ps34 = stack.enter_context(nc.psum_tensor("ps34", [128, 1024], F32))
        psum = [None] * 8
        for i in (0, 1, 2, 5, 6, 7):
            psum[i] = stack.enter_context(nc.psum_tensor("ps%d" % i, [128, 512], F32))
        psum[3] = ps34[:, 0:512]
        psum[4] = ps34[:, 512:1024]

        def B(n):
            return Buf(n)
        B_hT = [B("hT%d" % i) for i in range(NTB)]
        B_mT = [[B("mT") for _ in range(NTB)] for _ in range(8)]
        B_cb = [[B("cb") for _ in range(NTB)] for _ in range(8)]
        B_wr = [B("wr") for _ in range(NW)]
        B_wout, B_wa2 = B("wout"), B("wa2")
        B_x = [B("x%d" % i) for i in range(NXS)]
        B_xn = [B("xn%d" % i) for i in range(NXS)]
        B_gate, B_fg, B_vecs, B_cf, B_cbk = B("gate"), B("fg"), B("vecs"), B("cf"), B("cb")
        B_sm, B_scT, B_modT, B_s1, B_omu, B_grep = [B("sm%d" % i) for i in range(NXS)], B("scT"), B("modT"), B("s1"), B("omu"), B("grep")
        B_ub = [[B("ub") for _ in range(NTB + 1)] for _ in range(2)]
        B_ucar = [B("ucar") for _ in range(8)]
        B_dg = [B("dg0"), B("dg1")]
        B_acc1 = [B("acc1") for _ in range(NTB)]
        B_acc2 = [B("acc2") for _ in range(NTB)]
        B_t = [B("t%d" % i) for i in range(14)]
        B_b = [B("b%d" % i) for i in range(7)]
        B_Abl = [B("Abl") for _ in range(NTB + 1)]
        B_Ab = [[B("Ab") for _ in range(NTB + 1)] for _ in range(3)]
        B_abcar = B("abcar")
        B_ablc = B("ablc")
        B_la = [B("la") for _ in range(NTB)]
        B_Hst = [B("Hst") for _ in range(8)]
        B_Hbf = B("Hbf")
        B_PC2, B_ar2, B_VT2, B_btT2, B_ktT2 = ([B(nm + "0"), B(nm + "1")] for nm in ("PC", "ar", "VT", "btT", "ktT"))
        B_Ab42 = [[B("Ab4") for _ in range(8)] for _ in range(2)]
        B_Xf2 = [[B("Xf") for _ in range(8)] for _ in range(2)]
        B_Xt = [[B("Xt"), B("Xt")] for _ in range(8)]
        B_NT = [B("NT") for _ in range(8)]
        B_PQ = [[B("PQ"), B("PQ")] for _ in range(8)]
        B_RHS, B_Us = B("RHS"), B("Us")
        B_No2 = [[B("No") for _ in range(8)] for _ in range(2)]
        B_W1, B_R2 = B("W1"), B("R2")
        B_ps = [B("ps%d" % i) for i in range(8)]
        B_out = [B("out%d" % i) for i in range(NXS)]
        B_dbg = B("dbg")

        def V(name, j=0, w=1):
            o = VO[name] + j
            return vecs[:, o:o + w]

        ident_f = cf[:, 0:128]
        mask4 = cf[:, 128:640]
        slmask = cf[:, 640:768]
        rmask = cf[:, 768:1280]
        ones_f = cf[:, 1280:1408]
        offmask = cf[:, 1408:1536]
        ident_b = cb[:, 0:128]
        blockones = cb[:, 128:256]
        blockmean = cb[:, 256:384]
        ones_b = cb[:, 384:512]

        wkeys = {}

        def wsrc(key):
            kind, i = key
            return {"win": d_win, "wco": d_wco, "wro": d_wro}[kind][i]

        def program(P, sched):
            st = {"pos": 0, "dma": 0, "cast": 0}

            def wget(key):
                if P.dry:
                    sched.append(key)
                    return 0
                i = st["pos"]
                st["pos"] += 1
                assert sched[i] == key
                n = len(sched)
                while st["cast"] < min(n, i + WLA + 1):
                    j = st["cast"]
                    P.dma("pool", (lambda e, j=j: e.dma_start(out=wring[j % NW][:].rearrange("p a b -> p (a b)"),
                                                              in_=wsrc(sched[j]))),
                          reads=[], writes=[B_wr[j % NW]], key="wr%d" % (j % NW))
                    st["cast"] += 1
                return i % NW

            def act(out, in_, func, reads, writes, bias=None, scale=None, accum=None):
                kw = {}
                if bias is not None:
                    kw["bias"] = bias
                if scale is not None:
                    kw["scale"] = scale
                if accum is not None:
                    kw["accum_out"] = accum
                P.op("act", lambda e: e.activation(out=out, in_=in_, func=func, **kw), reads=reads, writes=writes)

            def tt(eng, out, in0, in1, op, reads, writes):
                P.op(eng, lambda e: e.tensor_tensor(out=out, in0=in0, in1=in1, op=op), reads=reads, writes=writes)

            def ts(eng, out, in0, s1_, s2_, op0, op1, reads, writes):
                if op1 is None:
                    P.op(eng, lambda e: e.tensor_scalar(out=out, in0=in0, scalar1=s1_, scalar2=None, op0=op0),
                         reads=reads, writes=writes)
                else:
                    P.op(eng, lambda e: e.tensor_scalar(out=out, in0=in0, scalar1=s1_, scalar2=s2_, op0=op0, op1=op1),
                         reads=reads, writes=writes)

            def stt(out, in0, sc, in1, op0, op1, reads, writes):
                P.op("dve", lambda e: e.scalar_tensor_tensor(out=out, in0=in0, scalar=sc, in1=in1, op0=op0, op1=op1),
                     reads=reads, writes=writes)

            def mm(out, lhsT, rhs, reads, writes, start=True, stop=True):
                P.op("pe", lambda e: e.matmul(out, lhsT, rhs, start=start, stop=stop), reads=reads, writes=writes)

            def tr(out, in_, ident, reads, writes):
                P.op("pe", lambda e: e.transpose(out, in_, ident), reads=reads, writes=writes)

            def proj(pi, slot, tb):
                for kc in range(KC):
                    mm(psum[pi][:, :], wring[slot][:, kc, :], hT[:, kc, tb * TB:(tb + 1) * TB],
                       reads=[B_wr[slot], B_hT[tb]], writes=[B_ps[pi]], start=(kc == 0), stop=(kc == KC - 1))

            def dump(name, src_ap, bufs):
                if dbg and name in d_dbg:
                    P.dma("sp", lambda e: e.dma_start(out=d_dbg[name], in_=src_ap), reads=bufs, writes=[B_dbg],
                          key="dbg")

            def shift_ap(kc, b):
                return modT[:, kc * nseq + b:kc * nseq + b + 1]

            def s1_ap(kc, b):
                return s1[:, kc * nseq + b:kc * nseq + b + 1]

            def gate_ap(dt, b):
                return modT[:, (16 + dt) * nseq + b:(16 + dt) * nseq + b + 1]

            def stageH(b, tok0, dumpit):
                alias_fence(T_bufs, X_bufs)
                for t_ in range(NTT):
                    s = t_ % NXS
                    r0 = tok0 + t_ * 128
                    sm = small[:, 4 * s:4 * s + 4]
                    P.dma("sp", (lambda e, s=s, r0=r0: e.dma_start(out=xring[s][:], in_=d_x[r0:r0 + 128, :])), [],
                          [B_x[s]], "x%d" % s)
                    act(xn[s][:], xring[s][:], AF.Square, [B_x[s]], [B_sm[s], B_xn[s]], accum=sm[:, 0:1])
                    act(sm[:, 1:2], sm[:, 0:1], AF.Sqrt, [B_sm[s]], [B_sm[s]], bias=1e-6, scale=1.0 / D)
                    P.op("dve", (lambda e, sm=sm: e.reciprocal(out=sm[:, 1:2], in_=sm[:, 1:2])), [B_sm[s]], [B_sm[s]])
                    ts("dve", xn[s][:], xring[s][:], sm[:, 1:2], None, ALU.mult, None,
                       [B_x[s], B_sm[s]], [B_xn[s]])
                    for half in range(2):
                        pi = 2 + (t_ * 2 + half) % 4
                        for q in range(4):
                            kc = half * 4 + q
                            tr(psum[pi][:, q * 128:(q + 1) * 128], xn[s][:, kc * 128:(kc + 1) * 128], ident_f,
                               [B_xn[s], B_cf], [B_ps[pi]])
                        for q in range(4):
                            kc = half * 4 + q
                            if q % 2 == 0:
                                act(hT[:, kc, t_ * 128:(t_ + 1) * 128], psum[pi][:, q * 128:(q + 1) * 128], AF.Identity,
                                    [B_ps[pi], B_s1, B_modT], [B_hT[t_ // 4]], bias=shift_ap(kc, b), scale=s1_ap(kc, b))
                            else:
                                ts("dve", hT[:, kc, t_ * 128:(t_ + 1) * 128], psum[pi][:, q * 128:(q + 1) * 128],
                                   s1_ap(kc, b), shift_ap(kc, b), ALU.mult, ALU.add,
                                   [B_ps[pi], B_s1, B_modT], [B_hT[t_ // 4]])
                if dumpit:
                    act(t512[8][:], hT[:, 0, 0:TB], AF.Identity, [B_hT[0]], [B_t[8]])
                    dump("hT", t512[8][:], [B_t[8]])

            def flat(x):
                out = []
                for y in x:
                    out.extend(flat(y) if isinstance(y, (list, tuple)) else [y])
                return out

            C_bufs = flat([B_ub, B_dg, B_acc1, B_acc2])
            R_bufs = flat([B_Abl, B_Ab, B_la, B_ar2, B_VT2, B_btT2, B_ktT2, B_Ab42, B_Xf2, B_Xt, B_NT, B_PQ, B_RHS, B_Us, B_No2, B_W1, B_R2])

            def alias_fence(src, dst):
                evs = []
                for b_ in src:
                    if b_.w is not None:
                        evs.append(b_.w)
                    evs.extend(b_.rs)
                for b_ in dst:
                    b_.rs = b_.rs + evs

            X_bufs = flat([B_x, B_xn])
            T_bufs = [B_t[i] for i in range(4 * NXS)]

            def stageC(first, dumpit):
                alias_fence(R_bufs, C_bufs)
                alias_fence(X_bufs, T_bufs)
                Fp = RPool(list(range(8)))
                items = [(ct, tb) for ct in range(8) for tb in range(NTB)]
                wsl = {}

                def c1_proj(ct, tb):
                    if tb == 0:
                        wsl[ct] = (wget(("win", CT_VAL + ct)), wget(("win", CT_GATE + ct)))
                    wv, wg = wsl[ct]
                    pv = Fp.get()
                    proj(pv, wv, tb)
                    pg = Fp.get()
                    proj(pg, wg, tb)
                    return pv, pg

                def c1_glu(ct, tb, pv, pg, k):
                    sl = ct % 2
                    if tb == 0:
                        for j in range(CW):
                            o = VO["conv_k"] + ct * CW + j
                            if j % 2 == 0:
                                act(dg[sl][:, j, :], ident_b, AF.Identity, [B_cbk, B_vecs], [B_dg[sl]],
                                    scale=vecs[:, o:o + 1])
                            else:
                                ts("dve", dg[sl][:, j, :], ident_b, vecs[:, o:o + 1], None, ALU.mult, None,
                                   [B_cbk, B_vecs], [B_dg[sl]])
                        if first:
                            P.op("pool", (lambda e: e.memset(ub[sl][:, 0:30], 0.0)), [], [B_ub[sl][0]])
                        else:
                            P.op("pool", (lambda e: e.tensor_copy(out=ub[sl][:, 0:30], in_=ucarry[:, ct, :])),
                                 [B_ucar[ct]], [B_ub[sl][0]])
                    sgt, Bsg = (b512[0], B_b[0]) if k % 2 == 0 else (b512[2], B_b[2])
                    act(sgt[:], psum[pg][:, :], AF.Sigmoid, [B_ps[pg]], [Bsg])
                    Fp.rel(pg)
                    tt("dve", ub[sl][:, 30 + tb * TB:30 + (tb + 1) * TB], psum[pv][:, :], sgt[:], ALU.mult,
                       [B_ps[pv], Bsg], [B_ub[sl][1 + tb]])
                    Fp.rel(pv)
                    if tb == NTB - 1:
                        P.op("pool", (lambda e: e.tensor_copy(out=ucarry[:, ct, :], in_=ub[sl][:, TS:TS + 30])),
                             [B_ub[sl][NTB]], [B_ucar[ct]])

                def c1_conv(ct, tb, prev):
                    sl = ct % 2
                    tsl = slice(tb * TB, (tb + 1) * TB)
                    pc = Fp.get()
                    for j in range(CW):
                        mm(psum[pc][:, :], dg[sl][:, j, :], ub[sl][:, tb * TB + j:tb * TB + j + TB],
                           [B_dg[sl], B_ub[sl][tb], B_ub[sl][1 + tb]], [B_ps[pc]], start=(j == 0), stop=(j == CW - 1))
                    if prev is not None:
                        c1_stats(*prev)
                    act(cbuf[:, ct, tsl], psum[pc][:, :], AF.Identity, [B_ps[pc], B_vecs],
                        [B_cb[ct][tb]], bias=V("conv_b", ct))
                    act(b512[1][:], psum[pc][:, :], AF.Square, [B_ps[pc], B_vecs], [B_b[1]], bias=V("conv_b", ct))
                    Fp.rel(pc)

                def c1_stats(ct, tb):
                    tsl = slice(tb * TB, (tb + 1) * TB)
                    p1 = Fp.get()
                    mm(psum[p1][:, :], ones_b, cbuf[:, ct, tsl], [B_cbk, B_cb[ct][tb]], [B_ps[p1]])
                    p2 = Fp.get()
                    mm(psum[p2][:, :], ones_b, b512[1][:], [B_cbk, B_b[1]], [B_ps[p2]])
                    a1 = acc1[:, tsl]
                    a2 = acc2[:, tsl]
                    if ct == 0:
                        act(a1, psum[p1][:, :], AF.Identity, [B_ps[p1]], [B_acc1[tb]])
                        act(a2, psum[p2][:, :], AF.Identity, [B_ps[p2]], [B_acc2[tb]])
                    else:
                        tt("dve", a1, psum[p1][:, :], a1, ALU.add, [B_ps[p1], B_acc1[tb]], [B_acc1[tb]])
                        tt("dve", a2, psum[p2][:, :], a2, ALU.add, [B_ps[p2], B_acc2[tb]], [B_acc2[tb]])
                    Fp.rel(p1)
                    Fp.rel(p2)

                nxt = c1_proj(*items[0])
                for k, (ct, tb) in enumerate(items):
                    pv, pg = nxt
                    c1_glu(ct, tb, pv, pg, k)
                    if k + 1 < len(items):
                        nxt = c1_proj(*items[k + 1])
                    c1_conv(ct, tb, items[k - 1] if k > 0 else None)
                c1_stats(*items[-1])
                if dumpit:
                    act(t512[0][:], cbuf[:, 0, 0:TB], AF.Identity, [B_cb[0][0]], [B_t[0]])
                    dump("conv", t512[0][:], [B_t[0]])
                for tb in range(NTB):
                    tsl = slice(tb * TB, (tb + 1) * TB)
                    a1 = acc1[:, tsl]
                    a2 = acc2[:, tsl]
                    ts("pool", a1, a1, 1.0 / D, None, ALU.mult, None, [B_acc1[tb]], [B_acc1[tb]])
                    tt("pool", t512[0][:], a1, a1, ALU.mult, [B_acc1[tb]], [B_t[0]])
                    stt(a2, a2, 1.0 / D, t512[0][:], ALU.mult, ALU.subtract, [B_acc2[tb], B_t[0]], [B_acc2[tb]])
                    act(a2, a2, AF.Ln, [B_acc2[tb]], [B_acc2[tb]], bias=1e-5)
                    act(a2, a2, AF.Exp, [B_acc2[tb]], [B_acc2[tb]], scale=-0.5)
                    stt(a1, a1, -1.0, a2, ALU.mult, ALU.mult, [B_acc1[tb], B_acc2[tb]], [B_acc1[tb]])
                for ct in range(8):
                    wo = wget(("win", CT_OG + ct))
                    for tb in range(NTB):
                        tsl = slice(tb * TB, (tb + 1) * TB)
                        a1 = acc1[:, tsl]
                        a2 = acc2[:, tsl]
                        po = Fp.get()
                        proj(po, wo, tb)
                        act(t512[1][:], psum[po][:, :], AF.Silu, [B_ps[po]], [B_t[1]])
                        Fp.rel(po)
                        cv = cbuf[:, ct, tsl]
                        tt("pool", t512[2][:], cv, a2, ALU.mult, [B_cb[ct][tb], B_acc2[tb]], [B_t[2]])
                        tt("pool", t512[2][:], t512[2][:], a1, ALU.add, [B_t[2], B_acc1[tb]], [B_t[2]])
                        act(t512[3][:], t512[2][:], AF.Silu, [B_t[2], B_vecs], [B_t[3]], bias=V("ln_b", ct),
                            scale=V("ln_g", ct))
                        tt("dve", cv, t512[3][:], t512[1][:], ALU.mult, [B_t[3], B_t[1]], [B_cb[ct][tb]])
                if dumpit:
                    act(t512[0][:], cbuf[:, 0, 0:TB], AF.Identity, [B_cb[0][0]], [B_t[0]])
                    dump("uc", t512[0][:], [B_t[0]])
                for dt in range(8):
                    wc = wget(("wco", dt))
                    wg = wget(("win", CT_GC + dt))
                    for tb in range(NTB):
                        tsl = slice(tb * TB, (tb + 1) * TB)
                        py = Fp.get()
                        for ct in range(8):
                            mm(psum[py][:, :], wring[wc][:, ct, :], cbuf[:, ct, tsl],
                               [B_wr[wc], B_cb[ct][tb]], [B_ps[py]], start=(ct == 0), stop=(ct == 7))
                        pg = Fp.get()
                        proj(pg, wg, tb)
                        act(t512[4][:], psum[pg][:, :], AF.Sigmoid, [B_ps[pg]], [B_t[4]])
                        Fp.rel(pg)
                        tt("dve", mT[:, dt, tsl], psum[py][:, :], t512[4][:], ALU.mult,
                           [B_ps[py], B_t[4]], [B_mT[dt][tb]])
                        Fp.rel(py)
                if dumpit:
                    act(t512[0][:], mT[:, 0, 0:TB], AF.Identity, [B_mT[0][0]], [B_t[0]])
                    dump("mconv", t512[0][:], [B_t[0]])

            def stageR(first, dumpit):
                alias_fence(C_bufs, R_bufs)
                Fp = RPool([0, 1])
                OTB = 2
                PQp = RPool([(3, 0), (3, 1), (4, 0), (4, 1)])
                Qx = RPool([(5, q) for q in range(4)])
                Qs = RPool([(6, q) for q in range(4)])
                Qm = RPool([(7, q) for q in range(4)])
                B_sub = {}

                def sub(bk, lo, n):
                    return psum[bk][:, lo:lo + n], B_ps[bk]

                def pq_ap(r):
                    bk, h = PQp.regs[r]
                    return sub(bk, h * 256, 256)

                def q_ap(pool, r):
                    bk, q = pool.regs[r]
                    return sub(bk, q * 128, 128)

                c3 = "p (c t) -> p c t"
                wl = wget(("win", CT_LORA))
                if first:
                    P.op("pool", lambda e: e.memset(Abl[:, 0:1], 0.0), [], [B_Abl[0]])
                else:
                    P.op("pool", lambda e: e.tensor_copy(out=Abl[:, 0:1], in_=ablc[:, 0:1]), [B_ablc], [B_Abl[0]])
                for tb in range(NTB):
                    tsl = slice(tb * TB, (tb + 1) * TB)
                    pl = Fp.get()
                    proj(pl, wl, tb)
                    act(Abl[:, 1 + tb * TB:1 + (tb + 1) * TB], psum[pl][:, :], AF.Identity, [B_ps[pl], B_vecs],
                        [B_Abl[1 + tb]], scale=V("mu", 24))
                    stt(t512[0][:], psum[pl][:, :], omu[:, 24:25], Abl[:, tb * TB:(tb + 1) * TB], ALU.mult, ALU.add,
                        [B_ps[pl], B_omu, B_Abl[tb], B_Abl[1 + tb]], [B_t[0]])
                    Fp.rel(pl)
                    if tb == NTB - 1:
                        P.op("pool", lambda e: e.tensor_copy(out=ablc[:, 0:1], in_=Abl[:, TS:TS + 1]), [B_Abl[NTB]],
                             [B_ablc])
                    act(la[0:64, tsl], t512[0][0:64, :], AF.Tanh, [B_t[0]], [B_la[tb]])
                    act(la[64:128, tsl], t512[0][64:128, :], AF.Identity, [B_t[0]], [B_la[tb]])

                blocks = [(hp, tb) for hp in range(8) for tb in range(NTB)]
                NB = len(blocks)
                hpw = {}
                r32, k32, ldr, cum, a32, kk, E1, E2, E3 = (t512[i] for i in range(9))
                Br32, Bk32, Bldr, Bcum, Ba32, Bkk, BE1, BE2, BE3 = (B_t[i] for i in range(9))
                rn, Brn, tmp, Btmp = ldr, Bldr, cum, Bcum
                bonl, Bbonl = [t512[9], t512[10]], [B_t[9], B_t[10]]
                o32, omean, ovar, osog = t512[11], t512[12], t512[13], t512[12]
                Bo32, Bomean, Bovar, Bosog = B_t[11], B_t[12], B_t[13], B_t[12]
                vbf, bt, kt, kk2, rkr = b512[0], b512[1], b512[2], b512[3], b512[4]
                Bvbf, Bbt, Bkt, Bkk2, Brkr = B_b[0], B_b[1], B_b[2], B_b[3], B_b[4]
                obf, osq, Bobf, Bosq = b512[5], b512[6], B_b[5], B_b[6]

                def gen_B(n):
                    hp, tb = blocks[n]
                    par = n % 2
                    ar, VT, btT, ktT, PCt = ar2[par], VT2[par], btT2[par], ktT2[par], PCt2[par]
                    B_ar, B_VT, B_btT, B_ktT, B_PC = B_ar2[par], B_VT2[par], B_btT2[par], B_ktT2[par], B_PC2[par]
                    Ab4, Xfin, No = Ab42[par], Xfin2[par], No2[par]
                    B_Ab4, B_Xf, B_No = B_Ab42[par], B_Xf2[par], B_No2[par]
                    bon, Bbon = bonl[par], Bbonl[par]
                    tsl = slice(tb * TB, (tb + 1) * TB)
                    hsl = slice(hp * 128, (hp + 1) * 128)
                    if tb == 0:
                        hpw[hp] = (wget(("win", CT_R + hp)), wget(("win", CT_K + hp)), wget(("win", CT_V + hp)),
                                   wget(("win", CT_ROG + hp)))
                        abz = [B_Ab[0][0], B_Ab[1][0], B_Ab[2][0]]
                        if first:
                            P.op("pool", lambda e: e.memset(Ab[:, :, 0:1], 0.0), [], abz)
                        else:
                            P.op("pool", (lambda e: e.tensor_copy(out=Ab[:, :, 0:1],
                                                                   in_=abcarry[:, hp, :].unsqueeze(2))),
                                 [B_abcar], abz)
                    wr_, wk_, wv_, wo_ = hpw[hp]
                    for j, (w_, dst, Bd, mi) in enumerate(((wr_, r32, Br32, hp), (wk_, k32, Bk32, 8 + hp),
                                                           (wv_, vbf, Bvbf, 16 + hp))):
                        yield from acquire(Fp, 1)
                        pp = Fp.get()
                        proj(pp, w_, tb)
                        act(Ab[:, j, 1 + tb * TB:1 + (tb + 1) * TB], psum[pp][:, :], AF.Identity,
                            [B_ps[pp], B_vecs], [B_Ab[j][1 + tb]], scale=V("mu", mi))
                        stt(dst[:], psum[pp][:, :], omu[:, mi:mi + 1], Ab[:, j, tb * TB:(tb + 1) * TB],
                            ALU.mult, ALU.add, [B_ps[pp], B_omu, B_Ab[j][tb], B_Ab[j][1 + tb]], [Bd])
                        Fp.rel(pp)
                        yield
                    if tb == NTB - 1:
                        P.op("pool", (lambda e: e.tensor_copy(out=abcarry[:, hp, :].unsqueeze(2),
                                                               in_=Ab[:, :, TS:TS + 1])),
                             [B_Ab[0][NTB], B_Ab[1][NTB], B_Ab[2][NTB]], [B_abcar])
                    yield from acquire(Fp, 1)
                    pw = Fp.get()
                    mm(psum[pw][:, :], wa2[0:64, hsl], la[0:64, tsl], [B_wa2, B_la[tb]], [B_ps[pw]])
                    act(ldr[:], psum[pw][:, :], AF.Sigmoid, [B_ps[pw], B_vecs], [Bldr], bias=V("w0", hp))
                    Fp.rel(pw)
                    yield from acquire(Fp, 1)
                    pa = Fp.get()
                    mm(psum[pa][:, :], wa2[64:128, hsl], la[64:128, tsl], [B_wa2, B_la[tb]], [B_ps[pa]])
                    act(a32[:], psum[pa][:, :], AF.Sigmoid, [B_ps[pa], B_vecs], [Ba32], bias=V("a0", hp))
                    Fp.rel(pa)
                    yield
                    P.op("dve", lambda e: e.tensor_tensor_scan(out=cum[:], data0=rmask, data1=ldr[:], initial=0.0,
                                                               op0=ALU.mult, op1=ALU.add),
                         [B_cf, Bldr], [Bcum])
                    tt("pool", ldr[:], cum[:], ldr[:], ALU.subtract, [Bcum, Bldr], [Bldr])
                    act(E1[:], cum[:], AF.Exp, [Bcum], [BE1], scale=-C0)
                    act(E3[:], cum[:], AF.Exp, [Bcum], [BE3], scale=C0)
                    act(E2[:], ldr[:], AF.Exp, [Bldr], [BE2], scale=-C0)
                    P.op("pool", lambda e: e.tensor_copy(out=PCt[:].unsqueeze(2),
                                                         in_=E1[:].rearrange(c3, t=CH)[:, :, CH - 1:CH]),
                         [BE1], [B_PC])
                    yield
                    act(kk[:], k32[:], AF.Identity, [Bk32, B_vecs], [Bkk], scale=V("k_k", hp))
                    act(kk2[:], k32[:], AF.Square, [Bk32, B_vecs], [Bkk2], scale=V("k_k", hp))
                    yield from acquire(Fp, 1)
                    pn = Fp.get()
                    mm(psum[pn][:, :], blockones, kk2[:], [B_cbk, Bkk2], [B_ps[pn]])
                    act(rn[:], psum[pn][:, :], AF.Ln, [B_ps[pn]], [Brn], bias=1e-24)
                    Fp.rel(pn)
                    act(rn[:], rn[:], AF.Exp, [Brn], [Brn], scale=-0.5)
                    tt("pool", kk[:], kk[:], rn[:], ALU.mult, [Bkk, Brn], [Bkk])
                    yield
                    act(tmp[:], a32[:], AF.Identity, [Ba32, B_vecs, B_omu], [Btmp], scale=V("k_a", hp),
                        bias=omka[:, hp:hp + 1])
                    tt("pool", k32[:], k32[:], tmp[:], ALU.mult, [Bk32, Btmp], [Bk32])
                    tt("pool", a32[:], kk[:], a32[:], ALU.mult, [Bkk, Ba32], [Ba32])
                    tt("dve", ar[:, :, 1, :], r32[:].rearrange(c3, t=CH), E1[:].rearrange(c3, t=CH), ALU.mult,
                       [Br32, BE1], [B_ar])
                    stt(ar[:, :, 0, :], kk[:].rearrange(c3, t=CH), -1.0, E2[:].rearrange(c3, t=CH),
                        ALU.mult, ALU.mult, [Bkk, BE2], [B_ar])
                    tt("dve", bt[:], a32[:], E3[:], ALU.mult, [Ba32, BE3], [Bbt])
                    tt("pool", kt[:], k32[:], E3[:], ALU.mult, [Bk32, BE3], [Bkt])
                    yield
                    stt(rkr[:], r32[:], V("r_k", hp), k32[:], ALU.mult, ALU.mult, [Br32, B_vecs, Bk32], [Brkr])
                    yield from acquire(Fp, 1)
                    pb = Fp.get()
                    mm(psum[pb][:, :], blockones, rkr[:], [B_cbk, Brkr], [B_ps[pb]])
                    tt("dve", bon[:], psum[pb][:, :], vbf[:], ALU.mult, [B_ps[pb], Bvbf], [Bbon])
                    Fp.rel(pb)
                    yield
                    for src, Bs, dst, Bd in ((vbf, Bvbf, VT, B_VT), (bt, Bbt, btT, B_btT), (kt, Bkt, ktT, B_ktT)):
                        yield from acquire(PQp, 1)
                        r = PQp.get()
                        pap, pB = pq_ap(r)
                        pbf = pap.bitcast(BF16)
                        for c in range(4):
                            tr(pbf[:, c * 128:(c + 1) * 128], src[:, c * CH:(c + 1) * CH], ident_b,
                               [Bs, B_cbk], [pB])
                        act(dst[:].rearrange("p c t -> p (c t)"), pbf, AF.Identity, [pB], [Bd])
                        PQp.rel(r)
                        yield
                def acquire(pool, k=1):
                    while sum(1 for h_ in pool.held if not h_) < k:
                        yield

                def gen_W(n, wave):
                    hp, tb = blocks[n]
                    par = n % 2
                    ar = ar2[par]
                    B_ar = B_ar2[par]
                    Ab4, Xfin, No = Ab42[par], Xfin2[par], No2[par]
                    B_Ab4, B_Xf, B_No = B_Ab42[par], B_Xf2[par], B_No2[par]
                    if True:
                        hcs = [(h, c) for c in (2 * wave, 2 * wave + 1) for h in (0, 1)]
                        cur = []
                        for i0, (h, c) in enumerate(hcs):
                            i = wave * 4 + i0
                            yield from acquire(Fp, 1)
                            yield from acquire(Qm, 1)
                            R = slice(h * 64, (h + 1) * 64)
                            g = h * 4 + c
                            csl = slice(c * CH, (c + 1) * CH)
                            arc = ar[R, c, :, :].rearrange("p a t -> p (a t)")
                            pA = Fp.get()
                            mm(psum[pA][:, 0:256], kt[R, csl], arc, [Bkt, B_ar], [B_ps[pA]])
                            mm(psum[pA][:, 256:512], bt[R, csl], arc, [Bbt, B_ar], [B_ps[pA]])
                            tt("dve", Ab4[g][:].rearrange("p a t -> p (a t)"), psum[pA][:, :], mask4, ALU.mult,
                               [B_ps[pA], B_cf], [B_Ab4[g]])
                            tt("dve", No[g][:], psum[pA][:, 256:384], offmask, ALU.mult, [B_ps[pA], B_cf], [B_No[g]])
                            Fp.rel(pA)
                            r = Qm.get()
                            qa, qB = q_ap(Qm, r)
                            mm(qa, ar[R, c, 0, :], bt[R, csl], [B_ar, Bbt], [qB])
                            tt("dve", NTb[i][:], qa, slmask, ALU.mult, [qB, B_cf], [B_NT[i]])
                            Qm.rel(r)
                            tt("pool", Xtmp[i][0][:], Ab4[g][:, 2, :], ident_b, ALU.add, [B_Ab4[g], B_cbk],
                               [B_Xt[i][0]])
                            cur.append(dict(P=Ab4[g][:, 2, :], BP=B_Ab4[g], Q=NTb[i][:], BQ=B_NT[i],
                                            X=Xtmp[i][0][:], BX=B_Xt[i][0], g=g, i=i))
                            yield
                        setup_done.add((n, wave))
                        w4 = slice(4 * wave, 4 * wave + 4)
                        gis = list(range(4 * wave, 4 * wave + 4))
                        for lv in range(1, 6):
                            last = lv == 5
                            par_ = lv % 2
                            yield from acquire(PQp, 4)
                            for i in range(4):
                                PQp.held[i] = True
                                s_ = cur[i]
                                pap, pB = pq_ap(i)
                                if not last:
                                    mm(pap[:, 0:128], s_["Q"], s_["P"], [s_["BQ"], s_["BP"]], [pB])
                                mm(pap[:, 128:256], s_["P"], s_["Q"], [s_["BQ"], s_["BP"]], [pB])
                            yield
                            wB = [B_PQ[gi][par_] for gi in gis]
                            if not last:
                                act(PQs_par[par_][:, w4, :].rearrange("p a b -> p (a b)"), ps34[:, :], AF.Identity,
                                    [B_ps[3], B_ps[4]], wB)
                            else:
                                act(PQs_par[par_][:, w4, 128:256],
                                    ps34[:, :].rearrange("p (a b) -> p a b", b=256)[:, :, 128:256], AF.Identity,
                                    [B_ps[3], B_ps[4]], wB)
                            for i in range(4):
                                PQp.rel(i)
                                s_ = cur[i]
                                gi = s_["i"]
                                dstT = PQs[gi][par_]
                                s_["P"], s_["BP"] = dstT[:, 0:128], B_PQ[gi][par_]
                                s_["Q"], s_["BQ"] = dstT[:, 128:256], B_PQ[gi][par_]
                            yield
                            yield from acquire(Qx, 4)
                            for i in range(4):
                                Qx.held[i] = True
                                s_ = cur[i]
                                qa, qB = q_ap(Qx, i)
                                mm(qa, s_["Q"], s_["X"], [s_["BQ"], s_["BX"]], [qB])
                            yield
                            xprev = Xtmp_par[1 - par_][:, w4, :].rearrange("p a b -> p (a b)")
                            Bprev = [B_Xt[gi][1 - par_] for gi in gis]
                            if last:
                                xout = Xfin_par[par][:, w4, :].rearrange("p a b -> p (a b)")
                                Bout = [B_Xf[cur[i]["g"]] for i in range(4)]
                            else:
                                xout = Xtmp_par[par_][:, w4, :].rearrange("p a b -> p (a b)")
                                Bout = [B_Xt[gi][par_] for gi in gis]
                            tt("dve", xout, psum[5][:, :], xprev, ALU.add, [B_ps[5]] + Bprev, Bout)
                            for i in range(4):
                                Qx.rel(i)
                                s_ = cur[i]
                                if last:
                                    s_["X"], s_["BX"] = Xfin[s_["g"]][:], B_Xf[s_["g"]]
                                else:
                                    s_["X"], s_["BX"] = Xtmp[s_["i"]][par_][:], B_Xt[s_["i"]][par_]
                            yield

                def gen_A(n):
                    hp, tb = blocks[n]
                    par = n % 2
                    ar, VT, btT, ktT, PCt = ar2[par], VT2[par], btT2[par], ktT2[par], PCt2[par]
                    B_ar, B_VT, B_btT, B_ktT, B_PC = B_ar2[par], B_VT2[par], B_btT2[par], B_ktT2[par], B_PC2[par]
                    Ab4, Xfin, No = Ab42[par], Xfin2[par], No2[par]
                    B_Ab4, B_Xf, B_No = B_Ab42[par], B_Xf2[par], B_No2[par]
                    bon, Bbon = bonl[par], Bbonl[par]
                    tsl = slice(tb * TB, (tb + 1) * TB)
                    H32 = Hst[:, hp, :]
                    BH = B_Hst[hp]
                    wo_ = hpw[hp][3]
                    if tb == 0:
                        if first:
                            P.op("pool", (lambda e: e.memset(H32, 0.0)), [], [BH])
                        act(Hbf[:], H32, AF.Identity, [BH], [B_Hbf])
                    pOT = psum[OTB]
                    for c in range(4):
                        r1 = Qs.get()
                        qa, qB = q_ap(Qs, r1)
                        for h in (0, 1):
                            R = slice(h * 64, (h + 1) * 64)
                            g = h * 4 + c
                            mm(qa[:, R], ar[R, c, 0, :], Hbf[R, :], [B_ar, B_Hbf], [qB], start=True, stop=False)
                            mm(qa[:, R], Ab4[g][:, 0, :], VT[:, c, R], [B_Ab4[g], B_VT], [qB], start=False, stop=True)
                        act(RHSs[:], qa, AF.Identity, [qB], [B_RHS])
                        Qs.rel(r1)
                        yield
                        r2 = Qs.get()
                        qa, qB = q_ap(Qs, r2)
                        for h in (0, 1):
                            R = slice(h * 64, (h + 1) * 64)
                            g = h * 4 + c
                            mm(qa[:, R], Xfin[g][:], RHSs[:, R], [B_Xf[g], B_RHS], [qB])
                        act(W1s[:], qa, AF.Identity, [qB], [B_W1])
                        Qs.rel(r2)
                        yield
                        r2 = Qs.get()
                        qa, qB = q_ap(Qs, r2)
                        for h in (0, 1):
                            R = slice(h * 64, (h + 1) * 64)
                            g = h * 4 + c
                            mm(qa[:, R], No[g][:], W1s[:, R], [B_No[g], B_W1], [qB])
                        tt("dve", R2s[:], qa, RHSs[:], ALU.add, [qB, B_RHS], [B_R2])
                        Qs.rel(r2)
                        yield
                        r2 = Qs.get()
                        qa, qB = q_ap(Qs, r2)
                        for h in (0, 1):
                            R = slice(h * 64, (h + 1) * 64)
                            g = h * 4 + c
                            mm(qa[:, R], Xfin[g][:], R2s[:, R], [B_Xf[g], B_R2], [qB])
                        act(Us[:], qa, AF.Identity, [qB], [B_Us])
                        Qs.rel(r2)
                        yield
                        r3 = Qs.get()
                        qa, qB = q_ap(Qs, r3)
                        for h in (0, 1):
                            R = slice(h * 64, (h + 1) * 64)
                            mm(qa[R, 0:64], btT[:, c, R], Us[:, R], [B_btT, B_Us], [qB], start=True, stop=False)
                            mm(qa[R, 0:64], ktT[:, c, R], VT[:, c, R], [B_ktT, B_VT], [qB], start=False, stop=True)
                        for h in (0, 1):
                            R = slice(h * 64, (h + 1) * 64)
                            g = h * 4 + c
                            o_ = pOT[R, c * CH:(c + 1) * CH]
                            mm(o_, Hbf[R, :], ar[R, c, 1, :], [B_Hbf, B_ar], [B_ps[OTB]], start=True, stop=False)
                            mm(o_, Us[:, R], Ab4[g][:, 3, :], [B_Us, B_Ab4[g]], [B_ps[OTB]], start=False, stop=False)
                            mm(o_, VT[:, c, R], Ab4[g][:, 1, :], [B_VT, B_Ab4[g]], [B_ps[OTB]], start=False, stop=True)
                        ts("dve", H32, H32, PCt[:, c:c + 1], None, ALU.mult, None, [BH, B_PC], [BH])
                        stt(Hbf[:], qa[:, 0:64], PCt[:, c:c + 1], H32, ALU.mult, ALU.add, [qB, B_PC, BH], [B_Hbf])
                        stt(H32, qa[:, 0:64], PCt[:, c:c + 1], H32, ALU.mult, ALU.add, [qB, B_PC, BH], [BH])
                        Qs.rel(r3)
                        yield
                    act(o32[:], pOT[:, :], AF.Identity, [B_ps[OTB]], [Bo32])
                    act(obf[:], pOT[:, :], AF.Identity, [B_ps[OTB]], [Bobf])
                    act(osq[:], pOT[:, :], AF.Square, [B_ps[OTB]], [Bosq])
                    if dumpit and hp == 0 and tb == 0:
                        dump("oscan", o32[:], [Bo32])
                    yield
                    yield from acquire(Fp, 1)
                    pm = Fp.get()
                    mm(psum[pm][:, :], blockmean, obf[:], [B_cbk, Bobf], [B_ps[pm]])
                    act(omean[:], psum[pm][:, :], AF.Identity, [B_ps[pm]], [Bomean])
                    Fp.rel(pm)
                    yield
                    yield from acquire(Fp, 1)
                    p2 = Fp.get()
                    mm(psum[p2][:, :], blockmean, osq[:], [B_cbk, Bosq], [B_ps[p2]])
                    tt("pool", ovar[:], omean[:], omean[:], ALU.mult, [Bomean], [Bovar])
                    tt("dve", ovar[:], psum[p2][:, :], ovar[:], ALU.subtract, [B_ps[p2], Bovar], [Bovar])
                    Fp.rel(p2)
                    act(ovar[:], ovar[:], AF.Ln, [Bovar], [Bovar], bias=64e-5)
                    act(ovar[:], ovar[:], AF.Exp, [Bovar], [Bovar], scale=-0.5)
                    yield
                    tt("pool", o32[:], o32[:], omean[:], ALU.subtract, [Bo32, Bomean], [Bo32])
                    tt("pool", o32[:], o32[:], ovar[:], ALU.mult, [Bo32, Bovar], [Bo32])
                    act(o32[:], o32[:], AF.Identity, [Bo32, B_vecs], [Bo32], scale=V("gn_g", hp), bias=V("gn_b", hp))
                    tt("pool", o32[:], o32[:], bon[:], ALU.add, [Bo32, Bbon], [Bo32])
                    yield
                    yield from acquire(Fp, 1)
                    po = Fp.get()
                    proj(po, wo_, tb)
                    act(osog[:], psum[po][:, :], AF.Silu, [B_ps[po]], [Bosog])
                    Fp.rel(po)
                    tt("dve", cbuf[:, hp, tsl], o32[:], osog[:], ALU.mult, [Bo32, Bosog], [B_cb[hp][tb]])
                    if dumpit and hp == 0 and tb == 0:
                        act(omean[:], cbuf[:, 0, 0:TB], AF.Identity, [B_cb[0][0]], [Bomean])
                        dump("og", omean[:], [Bomean])
                    yield

                setup_done = set()
                done = set()
                active = {}
                rate = {"A": 1, "B": 2, "W": 1}

                def can_start(kind, n, w=None):
                    if kind == "A":
                        return ("W", n, 0) in done and ("W", n, 1) in done and (n == 0 or ("A", n - 1) in done)
                    if kind == "B":
                        return ((n == 0 or ("B", n - 1) in done)
                                and (n == 0 or ((n - 1, 0) in setup_done and (n - 1, 1) in setup_done))
                                and (n < 2 or ("A", n - 2) in done))
                    if kind == "W":
                        return ("B", n) in done and (n == 0 or ("W", n - 1, w) in done)
                    return False

                started = set()
                guard = 0
                while len(done) < 4 * NB:
                    guard += 1
                    assert guard < 400000, "emission livelock"
                    for n in range(NB):
                        for key, mk in ((("B", n), lambda n=n: gen_B(n)), (("W", n, 0), lambda n=n: gen_W(n, 0)),
                                        (("W", n, 1), lambda n=n: gen_W(n, 1)), (("A", n), lambda n=n: gen_A(n))):
                            if key in started:
                                continue
                            if can_start(key[0], n, key[2] if len(key) > 2 else None):
                                active[key] = mk()
                                started.add(key)
                    for key in sorted(active, key=lambda k: (k[1], k[0])):
                        g = active[key]
                        for _ in range(rate[key[0]]):
                            try:
                                next(g)
                            except StopIteration:
                                del active[key]
                                done.add(key)
                                break

                for dt in range(8):
                    wr_ = wget(("wro", dt))
                    wg = wget(("win", CT_GR + dt))
                    for tb in range(NTB):
                        tsl = slice(tb * TB, (tb + 1) * TB)
                        py = Fp.get()
                        for hp in range(8):
                            mm(psum[py][:, :], wring[wr_][:, hp, :], cbuf[:, hp, tsl],
                               [B_wr[wr_], B_cb[hp][tb]], [B_ps[py]], start=(hp == 0), stop=(hp == 7))
                        pg = Fp.get()
                        proj(pg, wg, tb)
                        act(t512[4][:], psum[pg][:, :], AF.Sigmoid, [B_ps[pg]], [B_t[4]])
                        Fp.rel(pg)
                        tt("dve", t512[5][:], psum[py][:, :], t512[4][:], ALU.mult, [B_ps[py], B_t[4]], [B_t[5]])
                        Fp.rel(py)
                        tt("pool", mT[:, dt, tsl], mT[:, dt, tsl], t512[5][:], ALU.add, [B_mT[dt][tb], B_t[5]],
                           [B_mT[dt][tb]])
                if dumpit:
                    act(t512[0][:], mT[:, 0, 0:TB], AF.Identity, [B_mT[0][0]], [B_t[0]])
                    dump("m", t512[0][:], [B_t[0]])

            def stageF(b, tok0):
                alias_fence(T_bufs, X_bufs)
                for t_ in range(NTT):
                    s = t_ % NXS
                    r0 = tok0 + t_ * 128
                    tb = t_ // 4
                    sm = small[:, 4 * s:4 * s + 4]
                    P.dma("sp", (lambda e, s=s, r0=r0: e.dma_start(out=xring[s][:], in_=d_x[r0:r0 + 128, :])), [],
                          [B_x[s]], "x%d" % s)
                    for n in range(2):
                        pi = (t_ * 2 + n) % 8
                        for kc in range(KC):
                            mm(psum[pi][:, :], mT[:, kc, t_ * 128:(t_ + 1) * 128], wout[:, kc, n * 512:(n + 1) * 512],
                               [B_mT[kc][tb], B_woutk[kc]], [B_ps[pi]], start=(kc == 0), stop=(kc == KC - 1))
                        tt("dve", xn[s][:, n * 512:(n + 1) * 512], psum[pi][:, :], gate_bc[:, n * 512:(n + 1) * 512],
                           ALU.mult, [B_ps[pi], B_gate], [B_xn[s]])
                    tt("pool", xn[s][:], xn[s][:], xring[s][:], ALU.add, [B_xn[s], B_x[s]], [B_xn[s]])
                    act(xring[s][:], xn[s][:], AF.Square, [B_xn[s]], [B_sm[s], B_x[s]], accum=sm[:, 2:3])
                    act(sm[:, 3:4], sm[:, 2:3], AF.Sqrt, [B_sm[s]], [B_sm[s]], bias=1e-6, scale=1.0 / D)
                    P.op("dve", (lambda e, sm=sm: e.reciprocal(out=sm[:, 3:4], in_=sm[:, 3:4])), [B_sm[s]], [B_sm[s]])
                    stt(xring[s][:], xn[s][:], sm[:, 3:4], fg_bc[:], ALU.mult, ALU.mult, [B_xn[s], B_sm[s], B_fg],
                        [B_x[s]])
                    P.dma("sp", (lambda e, s=s, r0=r0: e.dma_start(out=d_out[r0:r0 + 128, :], in_=xring[s][:])),
                          [B_x[s]], [B_out[s]], "o%d" % s)

            P.dma("sp", lambda e: e.dma_start(out=vecs[:], in_=d_vecs), [], [B_vecs], "vecs")
            P.dma("sp", lambda e: e.dma_start(out=cf[:], in_=d_cf), [], [B_cf], "cf")
            P.dma("sp", lambda e: e.dma_start(out=cb[:], in_=d_cb), [], [B_cbk], "cbk")
            P.dma("sp", lambda e: e.dma_start(out=fg_bc[:], in_=d_fg), [], [B_fg], "fg")
            P.dma("sp", lambda e: e.dma_start(out=scT[:], in_=d_cT), [], [B_scT], "scT")
            B_woutk = [Buf("woutk%d" % kc) for kc in range(KC)]
            for kc in range(KC):
                P.dma("pool", (lambda e, kc=kc: e.dma_start(out=wout[:, kc, :], in_=d_wout[:, kc * D:(kc + 1) * D])),
                      [], [B_woutk[kc]], "wout%d" % kc)
            P.dma("pool", (lambda e: e.dma_start(out=wa2[:], in_=d_wa2)), [], [B_wa2], "wa2")
            act(scT[:], scT[:], AF.Silu, [B_scT], [B_scT])
            ts("pool", omu[:], V("mu", 0, 25), -1.0, 1.0, ALU.mult, ALU.add, [B_vecs], [B_omu])
            ts("pool", omka[:], V("k_a", 0, 8), -1.0, 1.0, ALU.mult, ALU.add, [B_vecs], [B_omu])
            for ct in range(24):
                s = ct % 2
                P.dma("sp", (lambda e, s=s, ct=ct: e.dma_start(out=xring[s][:], in_=d_adaw[ct])), [], [B_x[s]],
                      "x%d" % s)
                for kc in range(KC):
                    mm(psum[0][:, 0:nseq], xring[s][:, kc * 128:(kc + 1) * 128], scT[:, kc * nseq:(kc + 1) * nseq],
                       [B_x[s], B_scT], [B_ps[0]], start=(kc == 0), stop=(kc == KC - 1))
                act(modT[:, ct * nseq:(ct + 1) * nseq], psum[0][:, 0:nseq], AF.Identity, [B_ps[0], B_vecs], [B_modT],
                    bias=V("ada_b", ct))
            stt(s1[:], modT[:, 8 * nseq:16 * nseq], 1.0, V("ngrep", 0, 8 * nseq), ALU.add, ALU.mult,
                [B_modT, B_vecs], [B_s1])
            for b in range(nseq):
                for dt in range(8):
                    ts("dve", grep[:], ones_f, gate_ap(dt, b), None, ALU.mult, None, [B_cf, B_modT], [B_grep])
                    pq = psum[1][:, (dt % 4) * 128:(dt % 4 + 1) * 128]
                    mm(pq, grep[:], ident_f, [B_grep, B_cf], [B_ps[1]])
                    act(gate_bc[:, dt * 128:(dt + 1) * 128], pq, AF.Identity, [B_ps[1]], [B_gate])
                for sg in range(NSEG):
                    first = sg == 0
                    tok0 = b * T + sg * TS
                    dumpit = bool(dbg) and b == 0 and sg == 0
                    stageH(b, tok0, dumpit)
                    stageC(first, dumpit)
                    stageR(first, dumpit)
                    stageF(b, tok0)
            P.final_wait("sp", B_out + [B_dbg])

        sched = []
        program(Prog(nc, stack, dry=True), sched)
        P = Prog(nc, stack, dry=False)
        program(P, sched)
        P.emit()
        build_program.stats = dict(cnt=dict(P.cnt), nsem=P.nsem, nw=len(sched))
    return nc


def _tiles_lhsT(w):
    K, N = w.shape
    return np.ascontiguousarray(w.reshape(KC, 128, N // 128, 128).transpose(2, 1, 0, 3).reshape(N // 128, 128, KC * 128))


def _fm(v):
    return np.ascontiguousarray(v.reshape(-1, 128).T)


def host_layout(inp, seqs, nseq):
    f = np.float32
    VO, NV = vec_layout(nseq)
    sh = {}
    w_in = np.asarray(inp["w_in"][0], f)
    sh["win_t"] = _tiles_lhsT(w_in)
    sh["adaw_t"] = _tiles_lhsT(np.asarray(inp["ada_w"][0], f))
    sh["wco_t"] = _tiles_lhsT(np.asarray(inp["w_conv_out"][0], f))
    sh["wro_t"] = _tiles_lhsT(np.asarray(inp["w_rwkv_out"][0], f))
    sh["wout_r"] = np.ascontiguousarray(np.asarray(inp["w_out"][0], f).reshape(KC, 128, D).transpose(1, 0, 2).reshape(128, KC * D))
    sh["wa2"] = np.ascontiguousarray(np.concatenate([np.asarray(inp["rwkv_w2"][0], f), np.asarray(inp["rwkv_a2"][0], f)], 0))
    sh["fg_bc"] = np.ascontiguousarray(np.broadcast_to(np.asarray(inp["final_g"], f)[None, :], (128, D)))
    vec = np.zeros((128, NV), f)

    def put(name, arr):
        vec[:, VO[name]:VO[name] + arr.shape[1]] = arr
    put("ngrep", np.repeat(_fm(np.asarray(inp["norm_g"][0], f)), nseq, axis=1))
    put("ada_b", _fm(np.asarray(inp["ada_b"][0], f)))
    put("conv_b", _fm(np.asarray(inp["conv_b"][0], f)))
    put("ln_g", _fm(np.asarray(inp["conv_ln_g"][0], f)))
    put("ln_b", _fm(np.asarray(inp["conv_ln_b"][0], f)))
    ck = np.asarray(inp["conv_k"][0], f)
    put("conv_k", np.ascontiguousarray(ck.reshape(CW, 8, 128).transpose(2, 1, 0).reshape(128, 8 * CW)))
    put("mu", _fm(np.asarray(inp["rwkv_mu"][0], f)))
    for nm, key in (("w0", "rwkv_w0"), ("a0", "rwkv_a0"), ("k_k", "rwkv_k_k"), ("k_a", "rwkv_k_a"),
                    ("gn_g", "rwkv_gn_g"), ("gn_b", "rwkv_gn_b")):
        put(nm, _fm(np.asarray(inp[key][0], f)))
    put("r_k", _fm(np.asarray(inp["rwkv_r_k"][0], f).reshape(-1)))
    sh["vecs"] = vec
    cf = np.zeros((128, 1536), f)
    cf[:, 0:128] = np.eye(128, dtype=f)
    su = np.triu(np.ones((128, 128), f), 1)
    iu = np.triu(np.ones((128, 128), f), 0)
    b64 = np.kron(np.eye(2, dtype=f), np.ones((64, 64), f))
    cf[:, 128:640] = np.concatenate([su, iu, su * b64, iu], 1)
    cf[:, 640:768] = (su * b64).T
    cf[:, 1408:1536] = su * (1.0 - b64)
    rm = np.ones((128, TB), f)
    rm[:, ::CH] = 0.0
    cf[:, 768:1280] = rm
    cf[:, 1280:1408] = 1.0
    sh["cst_f"] = cf
    cbk = np.zeros((128, 512), f)
    cbk[:, 0:128] = np.eye(128, dtype=f)
    blk = np.kron(np.eye(2, dtype=f), np.ones((64, 64), f))
    cbk[:, 128:256] = blk
    cbk[:, 256:384] = blk / 64.0
    cbk[:, 384:512] = 1.0
    sh["cst_b"] = cbk.astype(ml_dtypes.bfloat16)
    x = np.asarray(inp["x"], f)
    c = np.asarray(inp["c"], f)
    maps = []
    for core_seqs in seqs:
        m = dict(sh)
        m["x"] = np.ascontiguousarray(x[core_seqs].reshape(-1, D))
        cc = c[core_seqs]
        m["cT"] = np.ascontiguousarray(cc.reshape(len(core_seqs), KC, 128).transpose(2, 1, 0).reshape(128, KC * len(core_seqs)))
        maps.append(m)
    return maps


_CACHE = {}


def kernel(**inputs):
    x = np.asarray(inputs["x"])
    Bsz, S, _ = x.shape
    ncores = 8
    nseq = Bsz // ncores
    key = (nseq, S)
    if key not in _CACHE:
        _CACHE[key] = build_program(nseq, S, 1024)
    nc = _CACHE[key]
    seqs = [list(range(i * nseq, (i + 1) * nseq)) for i in range(ncores)]
    maps = host_layout(inputs, seqs, nseq)
    res = run_bass_kernel_spmd(nc, maps, core_ids=list(range(ncores)))
    out = np.concatenate([np.asarray(r["out"]).reshape(nseq, S, D) for r in res.results], axis=0)
    return out.astype(np.float32)
```

```python
import contextlib
import numpy as np
import ml_dtypes
import concourse.bass as bass
import concourse.mybir as mybir
from concourse.bass_utils import run_bass_kernel_spmd

F32 = mybir.dt.float32
BF16 = mybir.dt.bfloat16
AF = mybir.ActivationFunctionType
ALU = mybir.AluOpType

D = 1024
KC = 8
CW = 31
NCT = 73
CT_VAL, CT_GATE, CT_OG, CT_R, CT_K, CT_V, CT_LORA, CT_ROG, CT_GC, CT_GR = 0, 8, 16, 24, 32, 40, 48, 49, 57, 65
C0 = float(np.exp(-0.5))
TB = 512
CH = 128

ENG = ("pe", "act", "dve", "pool", "sp")
EPOCH = 8000


class Buf:
    __slots__ = ("name", "w", "rs")

    def __init__(self, name):
        self.name = name
        self.w = None
        self.rs = []


class Prog:
    def __init__(self, nc, stack, dry=False):
        self.nc, self.stack, self.dry = nc, stack, dry
        self.q = {e: [] for e in ENG}
        self.cnt = {e: 0 for e in ENG}
        self.sems = {e: [] for e in ENG}
        self.seen = {e: {} for e in ENG}
        self.dsem = {}
        self.nsem = 0

    def _sem(self, name):
        self.nsem += 1
        return self.stack.enter_context(self.nc.semaphore(name))

    def _waits(self, e, reads, writes):
        evs = []
        for b in reads:
            if b.w is not None:
                evs.append((b.w, 0))
        for b in writes:
            if b.w is not None:
                evs.append((b.w, 1))
            for r in b.rs:
                evs.append((r, 2))
        waits = {}
        seen = self.seen[e]
        for (sem, val, eng, idx), kind in evs:
            if eng == e and e == "pe":
                continue
            key = id(sem)
            if seen.get(key, 0) >= val:
                continue
            if key in waits and waits[key][1] >= val:
                continue
            waits[key] = (sem, val)
        for key, (sem, val) in waits.items():
            seen[key] = val
        return list(waits.values())

    def _post(self, ev, reads, writes):
        for b in reads:
            b.rs.append(ev)
        for b in writes:
            b.w = ev
            b.rs = []

    def op(self, e, fn, reads=(), writes=()):
        if self.dry:
            return
        waits = self._waits(e, reads, writes)
        i = self.cnt[e]
        ep = i // EPOCH
        if ep >= len(self.sems[e]):
            self.sems[e].append(self._sem("s_%s_%d" % (e, ep)))
        sem = self.sems[e][ep]
        ev = (sem, i % EPOCH + 1, e, i)
        self.cnt[e] += 1
        self.q[e].append((waits, fn, sem, 1))
        self._post(ev, reads, writes)

    def dma(self, e, fn, reads, writes, key):
        if self.dry:
            return
        waits = self._waits(e, reads, writes)
        if key not in self.dsem:
            self.dsem[key] = [self._sem("d_" + key), 0]
        st = self.dsem[key]
        st[1] += 16
        ev = (st[0], st[1], None, -1)
        self.q[e].append((waits, fn, st[0], 16))
        self._post(ev, reads, writes)

    def final_wait(self, e, bufs):
        if self.dry:
            return
        waits = self._waits(e, bufs, bufs)
        self.q[e].append((waits, None, None, 0))

    def emit(self):
        nc = self.nc
        with nc.Block() as block:
            def mk(e):
                def body(eng):
                    for waits, fn, sem, inc in self.q[e]:
                        for s, v in waits:
                            eng.wait_ge(s, v)
                        if fn is not None:
                            fn(eng).then_inc(sem, inc)
                return body
            block.tensor(mk("pe"))
            block.scalar(mk("act"))
            block.vector(mk("dve"))
            block.gpsimd(mk("pool"))
            block.sync(mk("sp"))


class RPool:
    def __init__(self, regs):
        self.regs = regs
        self.held = [False] * len(regs)
        self.nxt = 0

    def get(self):
        n = len(self.regs)
        for k in range(n):
            i = (self.nxt + k) % n
            if not self.held[i]:
                self.held[i] = True
                self.nxt = (i + 1) % n
                return i
        raise RuntimeError("PSUM pool exhausted")

    def rel(self, i):
        assert self.held[i]
        self.held[i] = False


def vec_layout(nseq):
    off = {}
    n = 0
    for name, w in (("ngrep", 8 * nseq), ("ada_b", 24), ("conv_b", 8), ("ln_g", 8), ("ln_b", 8),
                    ("conv_k", 8 * CW), ("mu", 25), ("w0", 8), ("a0", 8), ("k_k", 8), ("k_a", 8),
                    ("r_k", 8), ("gn_g", 8), ("gn_b", 8)):
        off[name] = n
        n += w
    return off, n


def build_program(nseq, T, TS, dbg=None):
    assert T % TS == 0 and TS % TB == 0
    NTB = TS // TB
    NSEG = T // TS
    NTT = TS // 128
    NTOK = nseq * T
    VO, NV = vec_layout(nseq)
    nc = bass.Bass("TRN2", target_bir_lowering=False)

    def din(name, shape, dt=F32):
        return nc.dram_tensor(name, list(shape), dt, kind="ExternalInput").ap()

    d_x = din("x", [NTOK, D])
    d_cT = din("cT", [128, KC * nseq])
    d_adaw = din("adaw_t", [24, 128, D])
    d_win = din("win_t", [NCT, 128, D])
    d_wco = din("wco_t", [8, 128, D])
    d_wro = din("wro_t", [8, 128, D])
    d_wout = din("wout_r", [128, KC * D])
    d_wa2 = din("wa2", [128, D])
    d_vecs = din("vecs", [128, NV])
    d_fg = din("fg_bc", [128, D])
    d_cf = din("cst_f", [128, 1536])
    d_cb = din("cst_b", [128, 512], BF16)
    d_out = nc.dram_tensor("out", [NTOK, D], F32, kind="ExternalOutput").ap()
    d_dbg = {}
    if dbg:
        for name, shape in dbg.items():
            d_dbg[name] = nc.dram_tensor("dbg_" + name, list(shape), F32, kind="ExternalOutput").ap()

    stack = contextlib.ExitStack()
    with stack:
        def sb(name, shape, dt=F32):
            return stack.enter_context(nc.sbuf_tensor("sb_" + name, list(shape), dt))

        hT = sb("hT", [128, KC, TS], BF16)
        mT = sb("mT", [128, KC, TS], BF16)
        cbuf = sb("cbuf", [128, KC, TS], BF16)
        NW, WLA = 8, 3
        wring = [sb("wr%d" % i, [128, KC, 128], BF16) for i in range(NW)]
        wout = sb("wout", [128, KC, D], BF16)
        wa2 = sb("wa2b", [128, D], BF16)
        tbig = sb("tbig", [128, 14 * TB])
        NXS = 3
        xring = [tbig[:, i * D:(i + 1) * D] for i in range(NXS)]
        xn = [tbig[:, (NXS + i) * D:(NXS + 1 + i) * D] for i in range(NXS)]
        gate_bc = sb("gate_bc", [128, D])
        fg_bc = sb("fgbc", [128, D])
        vecs = sb("vecs", [128, NV])
        cf = sb("cstf", [128, 1536])
        cb = sb("cstb", [128, 512], BF16)
        small = sb("small", [128, 64])
        scT = sb("scT", [128, KC * nseq])
        modT = sb("modT", [128, 24 * nseq])
        s1 = sb("s1", [128, KC * nseq])
        omu = sb("omu", [128, 25])
        omka = sb("omka", [128, 8])
        grep = sb("grep", [128, 128])
        ARENA = 53504 + 7168
        arena = sb("arena", [128, ARENA // 2], BF16)
        carve_pos = [0]

        def carve(shape, dt=BF16):
            n = int(np.prod(shape[1:])) * (2 if dt == BF16 else 4)
            a = carve_pos[0]
            assert a % 4 == 0 and a + n <= ARENA, (a, n)
            carve_pos[0] = a + ((n + 3) // 4) * 4
            ap = arena[:, a // 2:(a + n) // 2]
            if dt != BF16:
                ap = ap.bitcast(dt)
            if len(shape) == 3:
                ap = ap.rearrange("p (a b) -> p a b", b=shape[2])
            elif len(shape) == 4:
                ap = ap.rearrange("p (a b c) -> p a b c", b=shape[2], c=shape[3])
            return ap
        ub = [carve([128, 30 + TS]) for i in range(2)]
        ucarry = sb("ucarry", [128, 8, 30], BF16)
        dg = [carve([128, CW, 128]) for i in range(2)]
        acc1 = carve([128, TS], F32)
        acc2 = carve([128, TS], F32)
        carve_pos[0] = 0
        t512 = [tbig[:, i * TB:(i + 1) * TB] for i in range(14)]
        b512 = [sb("b512_%d" % i, [128, TB], BF16) for i in range(7)]
        Abl = carve([128, 2 + TS])
        Ab = carve([128, 3, 2 + TS])
        abcarry = sb("abcarry", [128, 8, 3], BF16)
        ablc = sb("ablc", [128, 2], BF16)
        la = carve([128, TS])
        Hst = sb("Hst", [128, 8, 64])
        Hbf = sb("Hbf", [128, 64], BF16)
        PCt2 = [sb("PCt%d" % i, [128, 4]) for i in range(2)]
        ar2 = [carve([128, 4, 2, 128]) for _ in range(2)]
        VT2 = [carve([128, 4, 128]) for _ in range(2)]
        btT2 = [carve([128, 4, 128]) for _ in range(2)]
        ktT2 = [carve([128, 4, 128]) for _ in range(2)]
        Ab42 = [[carve([128, 4, 128]) for i in range(8)] for _ in range(2)]
        Xfin_par = [carve([128, 8, 128]) for _ in range(2)]
        Xfin2 = [[Xfin_par[p_][:, 2 * (g_ % 4) + g_ // 4, :] for g_ in range(8)] for p_ in range(2)]
        Xtmp_par = [carve([128, 8, 128]) for _ in range(2)]
        Xtmp = [[Xtmp_par[j][:, i, :] for j in range(2)] for i in range(8)]
        NTb = [carve([128, 128]) for i in range(8)]
        PQs_par = [carve([128, 8, 256]) for _ in range(2)]
        PQs = [[PQs_par[j][:, i, :] for j in range(2)] for i in range(8)]
        RHSs = carve([128, 128])
        Us = carve([128, 128])
        No2 = [[carve([128, 128]) for i in range(8)] for _ in range(2)]
        W1s = carve([128, 128])
        R2s = carve([128, 128])
        ps34 = stack.enter_context(nc.psum_tensor("ps34", [128, 1024], F32))
        psum = [None] * 8
        for i in (0, 1, 2, 5, 6, 7):
            psum[i] = stack.enter_context(nc.psum_tensor("ps%d" % i, [128, 512], F32))
        psum[3] = ps34[:, 0:512]
        psum[4] = ps34[:, 512:1024]

        def B(n):
            return Buf(n)
        B_hT = [B("hT%d" % i) for i in range(NTB)]
        B_mT = [[B("mT") for _ in range(NTB)] for _ in range(8)]
        B_cb = [[B("cb") for _ in range(NTB)] for _ in range(8)]
        B_wr = [B("wr") for _ in range(NW)]
        B_wout, B_wa2 = B("wout"), B("wa2")
        B_x = [B("x%d" % i) for i in range(NXS)]
        B_xn = [B("xn%d" % i) for i in range(NXS)]
        B_gate, B_fg, B_vecs, B_cf, B_cbk = B("gate"), B("fg"), B("vecs"), B("cf"), B("cb")
        B_sm, B_scT, B_modT, B_s1, B_omu, B_grep = [B("sm%d" % i) for i in range(NXS)], B("scT"), B("modT"), B("s1"), B("omu"), B("grep")
        B_ub = [[B("ub") for _ in range(NTB + 1)] for _ in range(2)]
        B_ucar = [B("ucar") for _ in range(8)]
        B_dg = [B("dg0"), B("dg1")]
        B_acc1 = [B("acc1") for _ in range(NTB)]
        B_acc2 = [B("acc2") for _ in range(NTB)]
        B_t = [B("t%d" % i) for i in range(14)]
        B_b = [B("b%d" % i) for i in range(7)]
        B_Abl = [B("Abl") for _ in range(NTB + 1)]
        B_Ab = [[B("Ab") for _ in range(NTB + 1)] for _ in range(3)]
        B_abcar = B("abcar")
        B_ablc = B("ablc")
        B_la = [B("la") for _ in range(NTB)]
        B_Hst = [B("Hst") for _ in range(8)]
        B_Hbf = B("Hbf")
        B_PC2, B_ar2, B_VT2, B_btT2, B_ktT2 = ([B(nm + "0"), B(nm + "1")] for nm in ("PC", "ar", "VT", "btT", "ktT"))
        B_Ab42 = [[B("Ab4") for _ in range(8)] for _ in range(2)]
        B_Xf2 = [[B("Xf") for _ in range(8)] for _ in range(2)]
        B_Xt = [[B("Xt"), B("Xt")] for _ in range(8)]
        B_NT = [B("NT") for _ in range(8)]
        B_PQ = [[B("PQ"), B("PQ")] for _ in range(8)]
        B_RHS, B_Us = B("RHS"), B("Us")
        B_No2 = [[B("No") for _ in range(8)] for _ in range(2)]
        B_W1, B_R2 = B("W1"), B("R2")
        B_ps = [B("ps%d" % i) for i in range(8)]
        B_out = [B("out%d" % i) for i in range(NXS)]
        B_dbg = B("dbg")

        def V(name, j=0, w=1):
            o = VO[name] + j
            return vecs[:, o:o + w]

        ident_f = cf[:, 0:128]
        mask4 = cf[:, 128:640]
        slmask = cf[:, 640:768]
        rmask = cf[:, 768:1280]
        ones_f = cf[:, 1280:1408]
        offmask = cf[:, 1408:1536]
        ident_b = cb[:, 0:128]
        blockones = cb[:, 128:256]
        blockmean = cb[:, 256:384]
        ones_b = cb[:, 384:512]

        wkeys = {}

        def wsrc(key):
            kind, i = key
            return {"win": d_win, "wco": d_wco, "wro": d_wro}[kind][i]

        def program(P, sched):
            st = {"pos": 0, "dma": 0, "cast": 0}

            def wget(key):
                if P.dry:
                    sched.append(key)
                    return 0
                i = st["pos"]
                st["pos"] += 1
                assert sched[i] == key
                n = len(sched)
                while st["cast"] < min(n, i + WLA + 1):
                    j = st["cast"]
                    P.dma("pool", (lambda e, j=j: e.dma_start(out=wring[j % NW][:].rearrange("p a b -> p (a b)"),
                                                              in_=wsrc(sched[j]))),
                          reads=[], writes=[B_wr[j % NW]], key="wr%d" % (j % NW))
                    st["cast"] += 1
                return i % NW

            def act(out, in_, func, reads, writes, bias=None, scale=None, accum=None):
                kw = {}
                if bias is not None:
                    kw["bias"] = bias
                if scale is not None:
                    kw["scale"] = scale
                if accum is not None:
                    kw["accum_out"] = accum
                P.op("act", lambda e: e.activation(out=out, in_=in_, func=func, **kw), reads=reads, writes=writes)

            def tt(eng, out, in0, in1, op, reads, writes):
                P.op(eng, lambda e: e.tensor_tensor(out=out, in0=in0, in1=in1, op=op), reads=reads, writes=writes)

            def ts(eng, out, in0, s1_, s2_, op0, op1, reads, writes):
                if op1 is None:
                    P.op(eng, lambda e: e.tensor_scalar(out=out, in0=in0, scalar1=s1_, scalar2=None, op0=op0),
                         reads=reads, writes=writes)
                else:
                    P.op(eng, lambda e: e.tensor_scalar(out=out, in0=in0, scalar1=s1_, scalar2=s2_, op0=op0, op1=op1),
                         reads=reads, writes=writes)

            def stt(out, in0, sc, in1, op0, op1, reads, writes):
                P.op("dve", lambda e: e.scalar_tensor_tensor(out=out, in0=in0, scalar=sc, in1=in1, op0=op0, op1=op1),
                     reads=reads, writes=writes)

            def mm(out, lhsT, rhs, reads, writes, start=True, stop=True):
                P.op("pe", lambda e: e.matmul(out, lhsT, rhs, start=start, stop=stop), reads=reads, writes=writes)

            def tr(out, in_, ident, reads, writes):
                P.op("pe", lambda e: e.transpose(out, in_, ident), reads=reads, writes=writes)

            def proj(pi, slot, tb):
                for kc in range(KC):
                    mm(psum[pi][:, :], wring[slot][:, kc, :], hT[:, kc, tb * TB:(tb + 1) * TB],
                       reads=[B_wr[slot], B_hT[tb]], writes=[B_ps[pi]], start=(kc == 0), stop=(kc == KC - 1))

            def dump(name, src_ap, bufs):
                if dbg and name in d_dbg:
                    P.dma("sp", lambda e: e.dma_start(out=d_dbg[name], in_=src_ap), reads=bufs, writes=[B_dbg],
                          key="dbg")

            def shift_ap(kc, b):
                return modT[:, kc * nseq + b:kc * nseq + b + 1]

            def s1_ap(kc, b):
                return s1[:, kc * nseq + b:kc * nseq + b + 1]

            def gate_ap(dt, b):
                return modT[:, (16 + dt) * nseq + b:(16 + dt) * nseq + b + 1]

            def stageH(b, tok0, dumpit):
                alias_fence(T_bufs, X_bufs)
                for t_ in range(NTT):
                    s = t_ % NXS
                    r0 = tok0 + t_ * 128
                    sm = small[:, 4 * s:4 * s + 4]
                    P.dma("sp", (lambda e, s=s, r0=r0: e.dma_start(out=xring[s][:], in_=d_x[r0:r0 + 128, :])), [],
                          [B_x[s]], "x%d" % s)
                    act(xn[s][:], xring[s][:], AF.Square, [B_x[s]], [B_sm[s], B_xn[s]], accum=sm[:, 0:1])
                    act(sm[:, 1:2], sm[:, 0:1], AF.Sqrt, [B_sm[s]], [B_sm[s]], bias=1e-6, scale=1.0 / D)
                    P.op("dve", (lambda e, sm=sm: e.reciprocal(out=sm[:, 1:2], in_=sm[:, 1:2])), [B_sm[s]], [B_sm[s]])
                    ts("dve", xn[s][:], xring[s][:], sm[:, 1:2], None, ALU.mult, None,
                       [B_x[s], B_sm[s]], [B_xn[s]])
                    for half in range(2):
                        pi = 2 + (t_ * 2 + half) % 4
                        for q in range(4):
                            kc = half * 4 + q
                            tr(psum[pi][:, q * 128:(q + 1) * 128], xn[s][:, kc * 128:(kc + 1) * 128], ident_f,
                               [B_xn[s], B_cf], [B_ps[pi]])
                        for q in range(4):
                            kc = half * 4 + q
                            if q % 2 == 0:
                                act(hT[:, kc, t_ * 128:(t_ + 1) * 128], psum[pi][:, q * 128:(q + 1) * 128], AF.Identity,
                                    [B_ps[pi], B_s1, B_modT], [B_hT[t_ // 4]], bias=shift_ap(kc, b), scale=s1_ap(kc, b))
                            else:
                                ts("dve", hT[:, kc, t_ * 128:(t_ + 1) * 128], psum[pi][:, q * 128:(q + 1) * 128],
                                   s1_ap(kc, b), shift_ap(kc, b), ALU.mult, ALU.add,
                                   [B_ps[pi], B_s1, B_modT], [B_hT[t_ // 4]])
                if dumpit:
                    act(t512[8][:], hT[:, 0, 0:TB], AF.Identity, [B_hT[0]], [B_t[8]])
                    dump("hT", t512[8][:], [B_t[8]])

            def flat(x):
                out = []
                for y in x:
                    out.extend(flat(y) if isinstance(y, (list, tuple)) else [y])
                return out

            C_bufs = flat([B_ub, B_dg, B_acc1, B_acc2])
            R_bufs = flat([B_Abl, B_Ab, B_la, B_ar2, B_VT2, B_btT2, B_ktT2, B_Ab42, B_Xf2, B_Xt, B_NT, B_PQ, B_RHS, B_Us, B_No2, B_W1, B_R2])

            def alias_fence(src, dst):
                evs = []
                for b_ in src:
                    if b_.w is not None:
                        evs.append(b_.w)
                    evs.extend(b_.rs)
                for b_ in dst:
                    b_.rs = b_.rs + evs

            X_bufs = flat([B_x, B_xn])
            T_bufs = [B_t[i] for i in range(4 * NXS)]

            def stageC(first, dumpit):
                alias_fence(R_bufs, C_bufs)
                alias_fence(X_bufs, T_bufs)
                Fp = RPool(list(range(8)))
                items = [(ct, tb) for ct in range(8) for tb in range(NTB)]
                wsl = {}

                def c1_proj(ct, tb):
                    if tb == 0:
                        wsl[ct] = (wget(("win", CT_VAL + ct)), wget(("win", CT_GATE + ct)))
                    wv, wg = wsl[ct]
                    pv = Fp.get()
                    proj(pv, wv, tb)
                    pg = Fp.get()
                    proj(pg, wg, tb)
                    return pv, pg

                def c1_glu(ct, tb, pv, pg, k):
                    sl = ct % 2
                    if tb == 0:
                        for j in range(CW):
                            o = VO["conv_k"] + ct * CW + j
                            if j % 2 == 0:
                                act(dg[sl][:, j, :], ident_b, AF.Identity, [B_cbk, B_vecs], [B_dg[sl]],
                                    scale=vecs[:, o:o + 1])
                            else:
                                ts("dve", dg[sl][:, j, :], ident_b, vecs[:, o:o + 1], None, ALU.mult, None,
                                   [B_cbk, B_vecs], [B_dg[sl]])
                        if first:
                            P.op("pool", (lambda e: e.memset(ub[sl][:, 0:30], 0.0)), [], [B_ub[sl][0]])
                        else:
                            P.op("pool", (lambda e: e.tensor_copy(out=ub[sl][:, 0:30], in_=ucarry[:, ct, :])),
                                 [B_ucar[ct]], [B_ub[sl][0]])
                    sgt, Bsg = (b512[0], B_b[0]) if k % 2 == 0 else (b512[2], B_b[2])
                    act(sgt[:], psum[pg][:, :], AF.Sigmoid, [B_ps[pg]], [Bsg])
                    Fp.rel(pg)
                    tt("dve", ub[sl][:, 30 + tb * TB:30 + (tb + 1) * TB], psum[pv][:, :], sgt[:], ALU.mult,
                       [B_ps[pv], Bsg], [B_ub[sl][1 + tb]])
                    Fp.rel(pv)
                    if tb == NTB - 1:
                        P.op("pool", (lambda e: e.tensor_copy(out=ucarry[:, ct, :], in_=ub[sl][:, TS:TS + 30])),
                             [B_ub[sl][NTB]], [B_ucar[ct]])

                def c1_conv(ct, tb, prev):
                    sl = ct % 2
                    tsl = slice(tb * TB, (tb + 1) * TB)
                    pc = Fp.get()
                    for j in range(CW):
                        mm(psum[pc][:, :], dg[sl][:, j, :], ub[sl][:, tb * TB + j:tb * TB + j + TB],
                           [B_dg[sl], B_ub[sl][tb], B_ub[sl][1 + tb]], [B_ps[pc]], start=(j == 0), stop=(j == CW - 1))
                    if prev is not None:
                        c1_stats(*prev)
                    act(cbuf[:, ct, tsl], psum[pc][:, :], AF.Identity, [B_ps[pc], B_vecs],
                        [B_cb[ct][tb]], bias=V("conv_b", ct))
                    act(b512[1][:], psum[pc][:, :], AF.Square, [B_ps[pc], B_vecs], [B_b[1]], bias=V("conv_b", ct))
                    Fp.rel(pc)

                def c1_stats(ct, tb):
                    tsl = slice(tb * TB, (tb + 1) * TB)
                    p1 = Fp.get()
                    mm(psum[p1][:, :], ones_b, cbuf[:, ct, tsl], [B_cbk, B_cb[ct][tb]], [B_ps[p1]])
                    p2 = Fp.get()
                    mm(psum[p2][:, :], ones_b, b512[1][:], [B_cbk, B_b[1]], [B_ps[p2]])
                    a1 = acc1[:, tsl]
                    a2 = acc2[:, tsl]
                    if ct == 0:
                        act(a1, psum[p1][:, :], AF.Identity, [B_ps[p1]], [B_acc1[tb]])
                        act(a2, psum[p2][:, :], AF.Identity, [B_ps[p2]], [B_acc2[tb]])
                    else:
                        tt("dve", a1, psum[p1][:, :], a1, ALU.add, [B_ps[p1], B_acc1[tb]], [B_acc1[tb]])
                        tt("dve", a2, psum[p2][:, :], a2, ALU.add, [B_ps[p2], B_acc2[tb]], [B_acc2[tb]])
                    Fp.rel(p1)
                    Fp.rel(p2)

                nxt = c1_proj(*items[0])
                for k, (ct, tb) in enumerate(items):
                    pv, pg = nxt
                    c1_glu(ct, tb, pv, pg, k)
                    if k + 1 < len(items):
                        nxt = c1_proj(*items[k + 1])
                    c1_conv(ct, tb, items[k - 1] if k > 0 else None)
                c1_stats(*items[-1])
                if dumpit:
                    act(t512[0][:], cbuf[:, 0, 0:TB], AF.Identity, [B_cb[0][0]], [B_t[0]])
                    dump("conv", t512[0][:], [B_t[0]])
                for tb in range(NTB):
                    tsl = slice(tb * TB, (tb + 1) * TB)
                    a1 = acc1[:, tsl]
                    a2 = acc2[:, tsl]
                    ts("pool", a1, a1, 1.0 / D, None, ALU.mult, None, [B_acc1[tb]], [B_acc1[tb]])
                    tt("pool", t512[0][:], a1, a1, ALU.mult, [B_acc1[tb]], [B_t[0]])
                    stt(a2, a2, 1.0 / D, t512[0][:], ALU.mult, ALU.subtract, [B_acc2[tb], B_t[0]], [B_acc2[tb]])
                    act(a2, a2, AF.Ln, [B_acc2[tb]], [B_acc2[tb]], bias=1e-5)
                    act(a2, a2, AF.Exp, [B_acc2[tb]], [B_acc2[tb]], scale=-0.5)
                    stt(a1, a1, -1.0, a2, ALU.mult, ALU.mult, [B_acc1[tb], B_acc2[tb]], [B_acc1[tb]])
                for ct in range(8):
                    wo = wget(("win", CT_OG + ct))
                    for tb in range(NTB):
                        tsl = slice(tb * TB, (tb + 1) * TB)
                        a1 = acc1[:, tsl]
                        a2 = acc2[:, tsl]
                        po = Fp.get()
                        proj(po, wo, tb)
                        act(t512[1][:], psum[po][:, :], AF.Silu, [B_ps[po]], [B_t[1]])
                        Fp.rel(po)
                        cv = cbuf[:, ct, tsl]
                        tt("pool", t512[2][:], cv, a2, ALU.mult, [B_cb[ct][tb], B_acc2[tb]], [B_t[2]])
                        tt("pool", t512[2][:], t512[2][:], a1, ALU.add, [B_t[2], B_acc1[tb]], [B_t[2]])
                        act(t512[3][:], t512[2][:], AF.Silu, [B_t[2], B_vecs], [B_t[3]], bias=V("ln_b", ct),
                            scale=V("ln_g", ct))
                        tt("dve", cv, t512[3][:], t512[1][:], ALU.mult, [B_t[3], B_t[1]], [B_cb[ct][tb]])
                if dumpit:
                    act(t512[0][:], cbuf[:, 0, 0:TB], AF.Identity, [B_cb[0][0]], [B_t[0]])
                    dump("uc", t512[0][:], [B_t[0]])
                for dt in range(8):
                    wc = wget(("wco", dt))
                    wg = wget(("win", CT_GC + dt))
                    for tb in range(NTB):
                        tsl = slice(tb * TB, (tb + 1) * TB)
                        py = Fp.get()
                        for ct in range(8):
                            mm(psum[py][:, :], wring[wc][:, ct, :], cbuf[:, ct, tsl],
                               [B_wr[wc], B_cb[ct][tb]], [B_ps[py]], start=(ct == 0), stop=(ct == 7))
                        pg = Fp.get()
                        proj(pg, wg, tb)
                        act(t512[4][:], psum[pg][:, :], AF.Sigmoid, [B_ps[pg]], [B_t[4]])
                        Fp.rel(pg)
                        tt("dve", mT[:, dt, tsl], psum[py][:, :], t512[4][:], ALU.mult,
                           [B_ps[py], B_t[4]], [B_mT[dt][tb]])
                        Fp.rel(py)
                if dumpit:
                    act(t512[0][:], mT[:, 0, 0:TB], AF.Identity, [B_mT[0][0]], [B_t[0]])
                    dump("mconv", t512[0][:], [B_t[0]])

            def stageR(first, dumpit):
                alias_fence(C_bufs, R_bufs)
                Fp = RPool([0, 1])
                OTB = 2
                PQp = RPool([(3, 0), (3, 1), (4, 0), (4, 1)])
                Qx = RPool([(5, q) for q in range(4)])
                Qs = RPool([(6, q) for q in range(4)])
                Qm = RPool([(7, q) for q in range(4)])
                B_sub = {}

                def sub(bk, lo, n):
                    return psum[bk][:, lo:lo + n], B_ps[bk]

                def pq_ap(r):
                    bk, h = PQp.regs[r]
                    return sub(bk, h * 256, 256)

                def q_ap(pool, r):
                    bk, q = pool.regs[r]
                    return sub(bk, q * 128, 128)

                c3 = "p (c t) -> p c t"
                wl = wget(("win", CT_LORA))
                if first:
                    P.op("pool", lambda e: e.memset(Abl[:, 0:1], 0.0), [], [B_Abl[0]])
                else:
                    P.op("pool", lambda e: e.tensor_copy(out=Abl[:, 0:1], in_=ablc[:, 0:1]), [B_ablc], [B_Abl[0]])
                for tb in range(NTB):
                    tsl = slice(tb * TB, (tb + 1) * TB)
                    pl = Fp.get()
                    proj(pl, wl, tb)
                    act(Abl[:, 1 + tb * TB:1 + (tb + 1) * TB], psum[pl][:, :], AF.Identity, [B_ps[pl], B_vecs],
                        [B_Abl[1 + tb]], scale=V("mu", 24))
                    stt(t512[0][:], psum[pl][:, :], omu[:, 24:25], Abl[:, tb * TB:(tb + 1) * TB], ALU.mult, ALU.add,
                        [B_ps[pl], B_omu, B_Abl[tb], B_Abl[1 + tb]], [B_t[0]])
                    Fp.rel(pl)
                    if tb == NTB - 1:
                        P.op("pool", lambda e: e.tensor_copy(out=ablc[:, 0:1], in_=Abl[:, TS:TS + 1]), [B_Abl[NTB]],
                             [B_ablc])
                    act(la[0:64, tsl], t512[0][0:64, :], AF.Tanh, [B_t[0]], [B_la[tb]])
                    act(la[64:128, tsl], t512[0][64:128, :], AF.Identity, [B_t[0]], [B_la[tb]])

                blocks = [(hp, tb) for hp in range(8) for tb in range(NTB)]
                NB = len(blocks)
                hpw = {}
                r32, k32, ldr, cum, a32, kk, E1, E2, E3 = (t512[i] for i in range(9))
                Br32, Bk32, Bldr, Bcum, Ba32, Bkk, BE1, BE2, BE3 = (B_t[i] for i in range(9))
                rn, Brn, tmp, Btmp = ldr, Bldr, cum, Bcum
                bonl, Bbonl = [t512[9], t512[10]], [B_t[9], B_t[10]]
                o32, omean, ovar, osog = t512[11], t512[12], t512[13], t512[12]
                Bo32, Bomean, Bovar, Bosog = B_t[11], B_t[12], B_t[13], B_t[12]
                vbf, bt, kt, kk2, rkr = b512[0], b512[1], b512[2], b512[3], b512[4]
                Bvbf, Bbt, Bkt, Bkk2, Brkr = B_b[0], B_b[1], B_b[2], B_b[3], B_b[4]
                obf, osq, Bobf, Bosq = b512[5], b512[6], B_b[5], B_b[6]

                def gen_B(n):
                    hp, tb = blocks[n]
                    par = n % 2
                    ar, VT, btT, ktT, PCt = ar2[par], VT2[par], btT2[par], ktT2[par], PCt2[par]
                    B_ar, B_VT, B_btT, B_ktT, B_PC = B_ar2[par], B_VT2[par], B_btT2[par], B_ktT2[par], B_PC2[par]
                    Ab4, Xfin, No = Ab42[par], Xfin2[par], No2[par]
                    B_Ab4, B_Xf, B_No = B_Ab42[par], B_Xf2[par], B_No2[par]
                    bon, Bbon = bonl[par], Bbonl[par]
                    tsl = slice(tb * TB, (tb + 1) * TB)
                    hsl = slice(hp * 128, (hp + 1) * 128)
                    if tb == 0:
                        hpw[hp] = (wget(("win", CT_R + hp)), wget(("win", CT_K + hp)), wget(("win", CT_V + hp)),
                                   wget(("win", CT_ROG + hp)))
                        abz = [B_Ab[0][0], B_Ab[1][0], B_Ab[2][0]]
                        if first:
                            P.op("pool", lambda e: e.memset(Ab[:, :, 0:1], 0.0), [], abz)
                        else:
                            P.op("pool", (lambda e: e.tensor_copy(out=Ab[:, :, 0:1],
                                                                   in_=abcarry[:, hp, :].unsqueeze(2))),
                                 [B_abcar], abz)
                    wr_, wk_, wv_, wo_ = hpw[hp]
                    for j, (w_, dst, Bd, mi) in enumerate(((wr_, r32, Br32, hp), (wk_, k32, Bk32, 8 + hp),
                                                           (wv_, vbf, Bvbf, 16 + hp))):
                        yield from acquire(Fp, 1)
                        pp = Fp.get()
                        proj(pp, w_, tb)
                        act(Ab[:, j, 1 + tb * TB:1 + (tb + 1) * TB], psum[pp][:, :], AF.Identity,
                            [B_ps[pp], B_vecs], [B_Ab[j][1 + tb]], scale=V("mu", mi))
                        stt(dst[:], psum[pp][:, :], omu[:, mi:mi + 1], Ab[:, j, tb * TB:(tb + 1) * TB],
                            ALU.mult, ALU.add, [B_ps[pp], B_omu, B_Ab[j][tb], B_Ab[j][1 + tb]], [Bd])
                        Fp.rel(pp)
                        yield
                    if tb == NTB - 1:
                        P.op("pool", (lambda e: e.tensor_copy(out=abcarry[:, hp, :].unsqueeze(2),
                                                               in_=Ab[:, :, TS:TS + 1])),
                             [B_Ab[0][NTB], B_Ab[1][NTB], B_Ab[2][NTB]], [B_abcar])
                    yield from acquire(Fp, 1)
                    pw = Fp.get()
                    mm(psum[pw][:, :], wa2[0:64, hsl], la[0:64, tsl], [B_wa2, B_la[tb]], [B_ps[pw]])
                    act(ldr[:], psum[pw][:, :], AF.Sigmoid, [B_ps[pw], B_vecs], [Bldr], bias=V("w0", hp))
                    Fp.rel(pw)
                    yield from acquire(Fp, 1)
                    pa = Fp.get()
                    mm(psum[pa][:, :], wa2[64:128, hsl], la[64:128, tsl], [B_wa2, B_la[tb]], [B_ps[pa]])
                    act(a32[:], psum[pa][:, :], AF.Sigmoid, [B_ps[pa], B_vecs], [Ba32], bias=V("a0", hp))
                    Fp.rel(pa)
                    yield
                    P.op("dve", lambda e: e.tensor_tensor_scan(out=cum[:], data0=rmask, data1=ldr[:], initial=0.0,
                                                               op0=ALU.mult, op1=ALU.add),
                         [B_cf, Bldr], [Bcum])
                    tt("pool", ldr[:], cum[:], ldr[:], ALU.subtract, [Bcum, Bldr], [Bldr])
                    act(E1[:], cum[:], AF.Exp, [Bcum], [BE1], scale=-C0)
                    act(E3[:], cum[:], AF.Exp, [Bcum], [BE3], scale=C0)
                    act(E2[:], ldr[:], AF.Exp, [Bldr], [BE2], scale=-C0)
                    P.op("pool", lambda e: e.tensor_copy(out=PCt[:].unsqueeze(2),
                                                         in_=E1[:].rearrange(c3, t=CH)[:, :, CH - 1:CH]),
                         [BE1], [B_PC])
                    yield
                    act(kk[:], k32[:], AF.Identity, [Bk32, B_vecs], [Bkk], scale=V("k_k", hp))
                    act(kk2[:], k32[:], AF.Square, [Bk32, B_vecs], [Bkk2], scale=V("k_k", hp))
                    yield from acquire(Fp, 1)
                    pn = Fp.get()
                    mm(psum[pn][:, :], blockones, kk2[:], [B_cbk, Bkk2], [B_ps[pn]])
                    act(rn[:], psum[pn][:, :], AF.Ln, [B_ps[pn]], [Brn], bias=1e-24)
                    Fp.rel(pn)
                    act(rn[:], rn[:], AF.Exp, [Brn], [Brn], scale=-0.5)
                    tt("pool", kk[:], kk[:], rn[:], ALU.mult, [Bkk, Brn], [Bkk])
                    yield
                    act(tmp[:], a32[:], AF.Identity, [Ba32, B_vecs, B_omu], [Btmp], scale=V("k_a", hp),
                        bias=omka[:, hp:hp + 1])
                    tt("pool", k32[:], k32[:], tmp[:], ALU.mult, [Bk32, Btmp], [Bk32])
                    tt("pool", a32[:], kk[:], a32[:], ALU.mult, [Bkk, Ba32], [Ba32])
                    tt("dve", ar[:, :, 1, :], r32[:].rearrange(c3, t=CH), E1[:].rearrange(c3, t=CH), ALU.mult,
                       [Br32, BE1], [B_ar])
                    stt(ar[:, :, 0, :], kk[:].rearrange(c3, t=CH), -1.0, E2[:].rearrange(c3, t=CH),
                        ALU.mult, ALU.mult, [Bkk, BE2], [B_ar])
                    tt("dve", bt[:], a32[:], E3[:], ALU.mult, [Ba32, BE3], [Bbt])
                    tt("pool", kt[:], k32[:], E3[:], ALU.mult, [Bk32, BE3], [Bkt])
                    yield
                    stt(rkr[:], r32[:], V("r_k", hp), k32[:], ALU.mult, ALU.mult, [Br32, B_vecs, Bk32], [Brkr])
                    yield from acquire(Fp, 1)
                    pb = Fp.get()
                    mm(psum[pb][:, :], blockones, rkr[:], [B_cbk, Brkr], [B_ps[pb]])
                    tt("dve", bon[:], psum[pb][:, :], vbf[:], ALU.mult, [B_ps[pb], Bvbf], [Bbon])
                    Fp.rel(pb)
                    yield
                    for src, Bs, dst, Bd in ((vbf, Bvbf, VT, B_VT), (bt, Bbt, btT, B_btT), (kt, Bkt, ktT, B_ktT)):
                        yield from acquire(PQp, 1)
                        r = PQp.get()
                        pap, pB = pq_ap(r)
                        pbf = pap.bitcast(BF16)
                        for c in range(4):
                            tr(pbf[:, c * 128:(c + 1) * 128], src[:, c * CH:(c + 1) * CH], ident_b,
                               [Bs, B_cbk], [pB])
                        act(dst[:].rearrange("p c t -> p (c t)"), pbf, AF.Identity, [pB], [Bd])
                        PQp.rel(r)
                        yield
                def acquire(pool, k=1):
                    while sum(1 for h_ in pool.held if not h_) < k:
                        yield

                def gen_W(n, wave):
                    hp, tb = blocks[n]
                    par = n % 2
                    ar = ar2[par]
                    B_ar = B_ar2[par]
                    Ab4, Xfin, No = Ab42[par], Xfin2[par], No2[par]
                    B_Ab4, B_Xf, B_No = B_Ab42[par], B_Xf2[par], B_No2[par]
                    if True:
                        hcs = [(h, c) for c in (2 * wave, 2 * wave + 1) for h in (0, 1)]
                        cur = []
                        for i0, (h, c) in enumerate(hcs):
                            i = wave * 4 + i0
                            yield from acquire(Fp, 1)
                            yield from acquire(Qm, 1)
                            R = slice(h * 64, (h + 1) * 64)
                            g = h * 4 + c
                            csl = slice(c * CH, (c + 1) * CH)
                            arc = ar[R, c, :, :].rearrange("p a t -> p (a t)")
                            pA = Fp.get()
                            mm(psum[pA][:, 0:256], kt[R, csl], arc, [Bkt, B_ar], [B_ps[pA]])
                            mm(psum[pA][:, 256:512], bt[R, csl], arc, [Bbt, B_ar], [B_ps[pA]])
                            tt("dve", Ab4[g][:].rearrange("p a t -> p (a t)"), psum[pA][:, :], mask4, ALU.mult,
                               [B_ps[pA], B_cf], [B_Ab4[g]])
                            tt("dve", No[g][:], psum[pA][:, 256:384], offmask, ALU.mult, [B_ps[pA], B_cf], [B_No[g]])
                            Fp.rel(pA)
                            r = Qm.get()
                            qa, qB = q_ap(Qm, r)
                            mm(qa, ar[R, c, 0, :], bt[R, csl], [B_ar, Bbt], [qB])
                            tt("dve", NTb[i][:], qa, slmask, ALU.mult, [qB, B_cf], [B_NT[i]])
                            Qm.rel(r)
                            tt("pool", Xtmp[i][0][:], Ab4[g][:, 2, :], ident_b, ALU.add, [B_Ab4[g], B_cbk],
                               [B_Xt[i][0]])
                            cur.append(dict(P=Ab4[g][:, 2, :], BP=B_Ab4[g], Q=NTb[i][:], BQ=B_NT[i],
                                            X=Xtmp[i][0][:], BX=B_Xt[i][0], g=g, i=i))
                            yield
                        setup_done.add((n, wave))
                        w4 = slice(4 * wave, 4 * wave + 4)
                        gis = list(range(4 * wave, 4 * wave + 4))
                        for lv in range(1, 6):
                            last = lv == 5
                            par_ = lv % 2
                            yield from acquire(PQp, 4)
                            for i in range(4):
                                PQp.held[i] = True
                                s_ = cur[i]
                                pap, pB = pq_ap(i)
                                if not last:
                                    mm(pap[:, 0:128], s_["Q"], s_["P"], [s_["BQ"], s_["BP"]], [pB])
                                mm(pap[:, 128:256], s_["P"], s_["Q"], [s_["BQ"], s_["BP"]], [pB])
                            yield
                            wB = [B_PQ[gi][par_] for gi in gis]
                            if not last:
                                act(PQs_par[par_][:, w4, :].rearrange("p a b -> p (a b)"), ps34[:, :], AF.Identity,
                                    [B_ps[3], B_ps[4]], wB)
                            else:
                                act(PQs_par[par_][:, w4, 128:256],
                                    ps34[:, :].rearrange("p (a b) -> p a b", b=256)[:, :, 128:256], AF.Identity,
                                    [B_ps[3], B_ps[4]], wB)
                            for i in range(4):
                                PQp.rel(i)
                                s_ = cur[i]
                                gi = s_["i"]
                                dstT = PQs[gi][par_]
                                s_["P"], s_["BP"] = dstT[:, 0:128], B_PQ[gi][par_]
                                s_["Q"], s_["BQ"] = dstT[:, 128:256], B_PQ[gi][par_]
                            yield
                            yield from acquire(Qx, 4)
                            for i in range(4):
                                Qx.held[i] = True
                                s_ = cur[i]
                                qa, qB = q_ap(Qx, i)
                                mm(qa, s_["Q"], s_["X"], [s_["BQ"], s_["BX"]], [qB])
                            yield
                            xprev = Xtmp_par[1 - par_][:, w4, :].rearrange("p a b -> p (a b)")
                            Bprev = [B_Xt[gi][1 - par_] for gi in gis]
                            if last:
                                xout = Xfin_par[par][:, w4, :].rearrange("p a b -> p (a b)")
                                Bout = [B_Xf[cur[i]["g"]] for i in range(4)]
                            else:
                                xout = Xtmp_par[par_][:, w4, :].rearrange("p a b -> p (a b)")
                                Bout = [B_Xt[gi][par_] for gi in gis]
                            tt("dve", xout, psum[5][:, :], xprev, ALU.add, [B_ps[5]] + Bprev, Bout)
                            for i in range(4):
                                Qx.rel(i)
                                s_ = cur[i]
                                if last:
                                    s_["X"], s_["BX"] = Xfin[s_["g"]][:], B_Xf[s_["g"]]
                                else:
                                    s_["X"], s_["BX"] = Xtmp[s_["i"]][par_][:], B_Xt[s_["i"]][par_]
                            yield

                def gen_A(n):
                    hp, tb = blocks[n]
                    par = n % 2
                    ar, VT, btT, ktT, PCt = ar2[par], VT2[par], btT2[par], ktT2[par], PCt2[par]
                    B_ar, B_VT, B_btT, B_ktT, B_PC = B_ar2[par], B_VT2[par], B_btT2[par], B_ktT2[par], B_PC2[par]
                    Ab4, Xfin, No = Ab42[par], Xfin2[par], No2[par]
                    B_Ab4, B_Xf, B_No = B_Ab42[par], B_Xf2[par], B_No2[par]
                    bon, Bbon = bonl[par], Bbonl[par]
                    tsl = slice(tb * TB, (tb + 1) * TB)
                    H32 = Hst[:, hp, :]
                    BH = B_Hst[hp]
                    wo_ = hpw[hp][3]
                    if tb == 0:
                        if first:
                            P.op("pool", (lambda e: e.memset(H32, 0.0)), [], [BH])
                        act(Hbf[:], H32, AF.Identity, [BH], [B_Hbf])
                    pOT = psum[OTB]
                    for c in range(4):
                        r1 = Qs.get()
                        qa, qB = q_ap(Qs, r1)
                        for h in (0, 1):
                            R = slice(h * 64, (h + 1) * 64)
                            g = h * 4 + c
                            mm(qa[:, R], ar[R, c, 0, :], Hbf[R, :], [B_ar, B_Hbf], [qB], start=True, stop=False)
                            mm(qa[:, R], Ab4[g][:, 0, :], VT[:, c, R], [B_Ab4[g], B_VT], [qB], start=False, stop=True)
                        act(RHSs[:], qa, AF.Identity, [qB], [B_RHS])
                        Qs.rel(r1)
                        yield
                        r2 = Qs.get()
                        qa, qB = q_ap(Qs, r2)
                        for h in (0, 1):
                            R = slice(h * 64, (h + 1) * 64)
                            g = h * 4 + c
                            mm(qa[:, R], Xfin[g][:], RHSs[:, R], [B_Xf[g], B_RHS], [qB])
                        act(W1s[:], qa, AF.Identity, [qB], [B_W1])
                        Qs.rel(r2)
                        yield
                        r2 = Qs.get()
                        qa, qB = q_ap(Qs, r2)
                        for h in (0, 1):
                            R = slice(h * 64, (h + 1) * 64)
                            g = h * 4 + c
                            mm(qa[:, R], No[g][:], W1s[:, R], [B_No[g], B_W1], [qB])
                        tt("dve", R2s[:], qa, RHSs[:], ALU.add, [qB, B_RHS], [B_R2])
                        Qs.rel(r2)
                        yield
                        r2 = Qs.get()
                        qa, qB = q_ap(Qs, r2)
                        for h in (0, 1):
                            R = slice(h * 64, (h + 1) * 64)
                            g = h * 4 + c
                            mm(qa[:, R], Xfin[g][:], R2s[:, R], [B_Xf[g], B_R2], [qB])
                        act(Us[:], qa, AF.Identity, [qB], [B_Us])
                        Qs.rel(r2)
                        yield
                        r3 = Qs.get()
                        qa, qB = q_ap(Qs, r3)
                        for h in (0, 1):
                            R = slice(h * 64, (h + 1) * 64)
                            mm(qa[R, 0:64], btT[:, c, R], Us[:, R], [B_btT, B_Us], [qB], start=True, stop=False)
                            mm(qa[R, 0:64], ktT[:, c, R], VT[:, c, R], [B_ktT, B_VT], [qB], start=False, stop=True)
                        for h in (0, 1):
                            R = slice(h * 64, (h + 1) * 64)
                            g = h * 4 + c
                            o_ = pOT[R, c * CH:(c + 1) * CH]
                            mm(o_, Hbf[R, :], ar[R, c, 1, :], [B_Hbf, B_ar], [B_ps[OTB]], start=True, stop=False)
                            mm(o_, Us[:, R], Ab4[g][:, 3, :], [B_Us, B_Ab4[g]], [B_ps[OTB]], start=False, stop=False)
                            mm(o_, VT[:, c, R], Ab4[g][:, 1, :], [B_VT, B_Ab4[g]], [B_ps[OTB]], start=False, stop=True)
                        ts("dve", H32, H32, PCt[:, c:c + 1], None, ALU.mult, None, [BH, B_PC], [BH])
                        stt(Hbf[:], qa[:, 0:64], PCt[:, c:c + 1], H32, ALU.mult, ALU.add, [qB, B_PC, BH], [B_Hbf])
                        stt(H32, qa[:, 0:64], PCt[:, c:c + 1], H32, ALU.mult, ALU.add, [qB, B_PC, BH], [BH])
                        Qs.rel(r3)
                        yield
                    act(o32[:], pOT[:, :], AF.Identity, [B_ps[OTB]], [Bo32])
                    act(obf[:], pOT[:, :], AF.Identity, [B_ps[OTB]], [Bobf])
                    act(osq[:], pOT[:, :], AF.Square, [B_ps[OTB]], [Bosq])
                    if dumpit and hp == 0 and tb == 0:
                        dump("oscan", o32[:], [Bo32])
                    yield
                    yield from acquire(Fp, 1)
                    pm = Fp.get()
                    mm(psum[pm][:, :], blockmean, obf[:], [B_cbk, Bobf], [B_ps[pm]])
                    act(omean[:], psum[pm][:, :], AF.Identity, [B_ps[pm]], [Bomean])
                    Fp.rel(pm)
                    yield
                    yield from acquire(Fp, 1)
                    p2 = Fp.get()
                    mm(psum[p2][:, :], blockmean, osq[:], [B_cbk, Bosq], [B_ps[p2]])
                    tt("pool", ovar[:], omean[:], omean[:], ALU.mult, [Bomean], [Bovar])
                    tt("dve", ovar[:], psum[p2][:, :], ovar[:], ALU.subtract, [B_ps[p2], Bovar], [Bovar])
                    Fp.rel(p2)
                    act(ovar[:], ovar[:], AF.Ln, [Bovar], [Bovar], bias=64e-5)
                    act(ovar[:], ovar[:], AF.Exp, [Bovar], [Bovar], scale=-0.5)
                    yield
                    tt("pool", o32[:], o32[:], omean[:], ALU.subtract, [Bo32, Bomean], [Bo32])
                    tt("pool", o32[:], o32[:], ovar[:], ALU.mult, [Bo32, Bovar], [Bo32])
                    act(o32[:], o32[:], AF.Identity, [Bo32, B_vecs], [Bo32], scale=V("gn_g", hp), bias=V("gn_b", hp))
                    tt("pool", o32[:], o32[:], bon[:], ALU.add, [Bo32, Bbon], [Bo32])
                    yield
                    yield from acquire(Fp, 1)
                    po = Fp.get()
                    proj(po, wo_, tb)
                    act(osog[:], psum[po][:, :], AF.Silu, [B_ps[po]], [Bosog])
                    Fp.rel(po)
                    tt("dve", cbuf[:, hp, tsl], o32[:], osog[:], ALU.mult, [Bo32, Bosog], [B_cb[hp][tb]])
                    if dumpit and hp == 0 and tb == 0:
                        act(omean[:], cbuf[:, 0, 0:TB], AF.Identity, [B_cb[0][0]], [Bomean])
                        dump("og", omean[:], [Bomean])
                    yield

                setup_done = set()
                done = set()
                active = {}
                rate = {"A": 1, "B": 1, "W": 1}

                def can_start(kind, n, w=None):
                    if kind == "A":
                        return ("W", n, 0) in done and ("W", n, 1) in done and (n == 0 or ("A", n - 1) in done)
                    if kind == "B":
                        return ((n == 0 or ("B", n - 1) in done)
                                and (n == 0 or ((n - 1, 0) in setup_done and (n - 1, 1) in setup_done))
                                and (n < 2 or ("A", n - 2) in done))
                    if kind == "W":
                        return ("B", n) in done and (n == 0 or ("W", n - 1, w) in done)
                    return False

                started = set()
                guard = 0
                while len(done) < 4 * NB:
                    guard += 1
                    assert guard < 400000, "emission livelock"
                    for n in range(NB):
                        for key, mk in ((("B", n), lambda n=n: gen_B(n)), (("W", n, 0), lambda n=n: gen_W(n, 0)),
                                        (("W", n, 1), lambda n=n: gen_W(n, 1)), (("A", n), lambda n=n: gen_A(n))):
                            if key in started:
                                continue
                            if can_start(key[0], n, key[2] if len(key) > 2 else None):
                                active[key] = mk()
                                started.add(key)
                    for key in sorted(active, key=lambda k: (k[1], k[0])):
                        g = active[key]
                        for _ in range(rate[key[0]]):
                            try:
                                next(g)
                            except StopIteration:
                                del active[key]
                                done.add(key)
                                break

                for dt in range(8):
                    wr_ = wget(("wro", dt))
                    wg = wget(("win", CT_GR + dt))
                    for tb in range(NTB):
                        tsl = slice(tb * TB, (tb + 1) * TB)
                        py = Fp.get()
                        for hp in range(8):
                            mm(psum[py][:, :], wring[wr_][:, hp, :], cbuf[:, hp, tsl],
                               [B_wr[wr_], B_cb[hp][tb]], [B_ps[py]], start=(hp == 0), stop=(hp == 7))
                        pg = Fp.get()
                        proj(pg, wg, tb)
                        act(t512[4][:], psum[pg][:, :], AF.Sigmoid, [B_ps[pg]], [B_t[4]])
                        Fp.rel(pg)
                        tt("dve", t512[5][:], psum[py][:, :], t512[4][:], ALU.mult, [B_ps[py], B_t[4]], [B_t[5]])
                        Fp.rel(py)
                        tt("pool", mT[:, dt, tsl], mT[:, dt, tsl], t512[5][:], ALU.add, [B_mT[dt][tb], B_t[5]],
                           [B_mT[dt][tb]])
                if dumpit:
                    act(t512[0][:], mT[:, 0, 0:TB], AF.Identity, [B_mT[0][0]], [B_t[0]])
                    dump("m", t512[0][:], [B_t[0]])

            def stageF(b, tok0):
                alias_fence(T_bufs, X_bufs)
                for t_ in range(NTT):
                    s = t_ % NXS
                    r0 = tok0 + t_ * 128
                    tb = t_ // 4
                    sm = small[:, 4 * s:4 * s + 4]
                    P.dma("sp", (lambda e, s=s, r0=r0: e.dma_start(out=xring[s][:], in_=d_x[r0:r0 + 128, :])), [],
                          [B_x[s]], "x%d" % s)
                    for n in range(2):
                        pi = (t_ * 2 + n) % 8
                        for kc in range(KC):
                            mm(psum[pi][:, :], mT[:, kc, t_ * 128:(t_ + 1) * 128], wout[:, kc, n * 512:(n + 1) * 512],
                               [B_mT[kc][tb], B_woutk[kc]], [B_ps[pi]], start=(kc == 0), stop=(kc == KC - 1))
                        tt("dve", xn[s][:, n * 512:(n + 1) * 512], psum[pi][:, :], gate_bc[:, n * 512:(n + 1) * 512],
                           ALU.mult, [B_ps[pi], B_gate], [B_xn[s]])
                    tt("pool", xn[s][:], xn[s][:], xring[s][:], ALU.add, [B_xn[s], B_x[s]], [B_xn[s]])
                    act(xring[s][:], xn[s][:], AF.Square, [B_xn[s]], [B_sm[s], B_x[s]], accum=sm[:, 2:3])
                    act(sm[:, 3:4], sm[:, 2:3], AF.Sqrt, [B_sm[s]], [B_sm[s]], bias=1e-6, scale=1.0 / D)
                    P.op("dve", (lambda e, sm=sm: e.reciprocal(out=sm[:, 3:4], in_=sm[:, 3:4])), [B_sm[s]], [B_sm[s]])
                    stt(xring[s][:], xn[s][:], sm[:, 3:4], fg_bc[:], ALU.mult, ALU.mult, [B_xn[s], B_sm[s], B_fg],
                        [B_x[s]])
                    P.dma("sp", (lambda e, s=s, r0=r0: e.dma_start(out=d_out[r0:r0 + 128, :], in_=xring[s][:])),
                          [B_x[s]], [B_out[s]], "o%d" % s)

            P.dma("sp", lambda e: e.dma_start(out=vecs[:], in_=d_vecs), [], [B_vecs], "vecs")
            P.dma("sp", lambda e: e.dma_start(out=cf[:], in_=d_cf), [], [B_cf], "cf")
            P.dma("sp", lambda e: e.dma_start(out=cb[:], in_=d_cb), [], [B_cbk], "cbk")
            P.dma("sp", lambda e: e.dma_start(out=fg_bc[:], in_=d_fg), [], [B_fg], "fg")
            P.dma("sp", lambda e: e.dma_start(out=scT[:], in_=d_cT), [], [B_scT], "scT")
            B_woutk = [Buf("woutk%d" % kc) for kc in range(KC)]
            for kc in range(KC):
                P.dma("pool", (lambda e, kc=kc: e.dma_start(out=wout[:, kc, :], in_=d_wout[:, kc * D:(kc + 1) * D])),
                      [], [B_woutk[kc]], "wout%d" % kc)
            P.dma("pool", (lambda e: e.dma_start(out=wa2[:], in_=d_wa2)), [], [B_wa2], "wa2")
            act(scT[:], scT[:], AF.Silu, [B_scT], [B_scT])
            ts("pool", omu[:], V("mu", 0, 25), -1.0, 1.0, ALU.mult, ALU.add, [B_vecs], [B_omu])
            ts("pool", omka[:], V("k_a", 0, 8), -1.0, 1.0, ALU.mult, ALU.add, [B_vecs], [B_omu])
            for ct in range(24):
                s = ct % 2
                P.dma("sp", (lambda e, s=s, ct=ct: e.dma_start(out=xring[s][:], in_=d_adaw[ct])), [], [B_x[s]],
                      "x%d" % s)
                for kc in range(KC):
                    mm(psum[0][:, 0:nseq], xring[s][:, kc * 128:(kc + 1) * 128], scT[:, kc * nseq:(kc + 1) * nseq],
                       [B_x[s], B_scT], [B_ps[0]], start=(kc == 0), stop=(kc == KC - 1))
                act(modT[:, ct * nseq:(ct + 1) * nseq], psum[0][:, 0:nseq], AF.Identity, [B_ps[0], B_vecs], [B_modT],
                    bias=V("ada_b", ct))
            stt(s1[:], modT[:, 8 * nseq:16 * nseq], 1.0, V("ngrep", 0, 8 * nseq), ALU.add, ALU.mult,
                [B_modT, B_vecs], [B_s1])
            for b in range(nseq):
                for dt in range(8):
                    ts("dve", grep[:], ones_f, gate_ap(dt, b), None, ALU.mult, None, [B_cf, B_modT], [B_grep])
                    pq = psum[1][:, (dt % 4) * 128:(dt % 4 + 1) * 128]
                    mm(pq, grep[:], ident_f, [B_grep, B_cf], [B_ps[1]])
                    act(gate_bc[:, dt * 128:(dt + 1) * 128], pq, AF.Identity, [B_ps[1]], [B_gate])
                for sg in range(NSEG):
                    first = sg == 0
                    tok0 = b * T + sg * TS
                    dumpit = bool(dbg) and b == 0 and sg == 0
                    stageH(b, tok0, dumpit)
                    stageC(first, dumpit)
                    stageR(first, dumpit)
                    stageF(b, tok0)
            P.final_wait("sp", B_out + [B_dbg])

        sched = []
        program(Prog(nc, stack, dry=True), sched)
        P = Prog(nc, stack, dry=False)
        program(P, sched)
        P.emit()
        build_program.stats = dict(cnt=dict(P.cnt), nsem=P.nsem, nw=len(sched))
    return nc


def _tiles_lhsT(w):
    K, N = w.shape
    return np.ascontiguousarray(w.reshape(KC, 128, N // 128, 128).transpose(2, 1, 0, 3).reshape(N // 128, 128, KC * 128))


def _fm(v):
    return np.ascontiguousarray(v.reshape(-1, 128).T)


def host_layout(inp, seqs, nseq):
    f = np.float32
    VO, NV = vec_layout(nseq)
    sh = {}
    w_in = np.asarray(inp["w_in"][0], f)
    sh["win_t"] = _tiles_lhsT(w_in)
    sh["adaw_t"] = _tiles_lhsT(np.asarray(inp["ada_w"][0], f))
    sh["wco_t"] = _tiles_lhsT(np.asarray(inp["w_conv_out"][0], f))
    sh["wro_t"] = _tiles_lhsT(np.asarray(inp["w_rwkv_out"][0], f))
    sh["wout_r"] = np.ascontiguousarray(np.asarray(inp["w_out"][0], f).reshape(KC, 128, D).transpose(1, 0, 2).reshape(128, KC * D))
    sh["wa2"] = np.ascontiguousarray(np.concatenate([np.asarray(inp["rwkv_w2"][0], f), np.asarray(inp["rwkv_a2"][0], f)], 0))
    sh["fg_bc"] = np.ascontiguousarray(np.broadcast_to(np.asarray(inp["final_g"], f)[None, :], (128, D)))
    vec = np.zeros((128, NV), f)

    def put(name, arr):
        vec[:, VO[name]:VO[name] + arr.shape[1]] = arr
    put("ngrep", np.repeat(_fm(np.asarray(inp["norm_g"][0], f)), nseq, axis=1))
    put("ada_b", _fm(np.asarray(inp["ada_b"][0], f)))
    put("conv_b", _fm(np.asarray(inp["conv_b"][0], f)))
    put("ln_g", _fm(np.asarray(inp["conv_ln_g"][0], f)))
    put("ln_b", _fm(np.asarray(inp["conv_ln_b"][0], f)))
    ck = np.asarray(inp["conv_k"][0], f)
    put("conv_k", np.ascontiguousarray(ck.reshape(CW, 8, 128).transpose(2, 1, 0).reshape(128, 8 * CW)))
    put("mu", _fm(np.asarray(inp["rwkv_mu"][0], f)))
    for nm, key in (("w0", "rwkv_w0"), ("a0", "rwkv_a0"), ("k_k", "rwkv_k_k"), ("k_a", "rwkv_k_a"),
                    ("gn_g", "rwkv_gn_g"), ("gn_b", "rwkv_gn_b")):
        put(nm, _fm(np.asarray(inp[key][0], f)))
    put("r_k", _fm(np.asarray(inp["rwkv_r_k"][0], f).reshape(-1)))
    sh["vecs"] = vec
    cf = np.zeros((128, 1536), f)
    cf[:, 0:128] = np.eye(128, dtype=f)
    su = np.triu(np.ones((128, 128), f), 1)
    iu = np.triu(np.ones((128, 128), f), 0)
    b64 = np.kron(np.eye(2, dtype=f), np.ones((64, 64), f))
    cf[:, 128:640] = np.concatenate([su, iu, su * b64, iu], 1)
    cf[:, 640:768] = (su * b64).T
    cf[:, 1408:1536] = su * (1.0 - b64)
    rm = np.ones((128, TB), f)
    rm[:, ::CH] = 0.0
    cf[:, 768:1280] = rm
    cf[:, 1280:1408] = 1.0
    sh["cst_f"] = cf
    cbk = np.zeros((128, 512), f)
    cbk[:, 0:128] = np.eye(128, dtype=f)
    blk = np.kron(np.eye(2, dtype=f), np.ones((64, 64), f))
    cbk[:, 128:256] = blk
    cbk[:, 256:384] = blk / 64.0
    cbk[:, 384:512] = 1.0
    sh["cst_b"] = cbk.astype(ml_dtypes.bfloat16)
    x = np.asarray(inp["x"], f)
    c = np.asarray(inp["c"], f)
    maps = []
    for core_seqs in seqs:
        m = dict(sh)
        m["x"] = np.ascontiguousarray(x[core_seqs].reshape(-1, D))
        cc = c[core_seqs]
        m["cT"] = np.ascontiguousarray(cc.reshape(len(core_seqs), KC, 128).transpose(2, 1, 0).reshape(128, KC * len(core_seqs)))
        maps.append(m)
    return maps


_CACHE = {}


def kernel(**inputs):
    x = np.asarray(inputs["x"])
    Bsz, S, _ = x.shape
    ncores = 8
    nseq = Bsz // ncores
    key = (nseq, S)
    if key not in _CACHE:
        _CACHE[key] = build_program(nseq, S, 1024)
    nc = _CACHE[key]
    seqs = [list(range(i * nseq, (i + 1) * nseq)) for i in range(ncores)]
    maps = host_layout(inputs, seqs, nseq)
    res = run_bass_kernel_spmd(nc, maps, core_ids=list(range(ncores)))
    out = np.concatenate([np.asarray(r["out"]).reshape(nseq, S, D) for r in res.results], axis=0)
    return out.astype(np.float32)
```
